# Optimizing a Trainium2 kernel written in Bass

```python
import jax, jax.numpy as jnp
from jax import lax
import numpy as np

D_MODEL = 1024
BATCH = 4
SEQ = 4096
DEPTH = 1
DEC_BATCH = 128
DEC_SEQ = 4
PAST_LEN = 16384
PAGE_SIZE = 128

D_MIX = D_MODEL
D_ATTN = D_MIX // 2
D_GMLP = D_MIX - D_ATTN
HEAD_DIM = 64
N_HEADS = D_ATTN // HEAD_DIM
N_KV = 2
GQA = N_HEADS // N_KV
WINDOW = 128
ROPE_THETA = 10000.0
CHUNK = 128
G_DIM = 64
G_HEADS = D_GMLP // G_DIM
D_FF = 2816
EPS = 1e-6
Q_W = N_HEADS * HEAD_DIM
KV_W = N_KV * HEAD_DIM
D_IN = Q_W + 2 * KV_W + 2 * D_GMLP

kernel_name = "hymba_swa_sink_gmlp_macaron_step"


def _rmsnorm(x, g):
    xf = x.astype(jnp.float32)
    y = xf * lax.rsqrt(jnp.mean(xf * xf, axis=-1, keepdims=True) + EPS)
    return (y * g.astype(jnp.float32)).astype(x.dtype)


def _rope(x, pos):
    inv_freq = ROPE_THETA ** (-jnp.arange(0, HEAD_DIM, 2, dtype=jnp.float32) / HEAD_DIM)
    ang = pos.astype(jnp.float32)[:, None] * inv_freq[None, :]
    c = jnp.cos(ang)[:, None, :]
    s = jnp.sin(ang)[:, None, :]
    xf = x.astype(jnp.float32)
    x1, x2 = xf[..., :HEAD_DIM // 2], xf[..., HEAD_DIM // 2:]
    return jnp.concatenate([x1 * c - x2 * s, x2 * c + x1 * s], axis=-1).astype(x.dtype)


def _ffn_half(x, g, wg, wu, wd):
    h = _rmsnorm(x, g)
    return x + 0.5 * ((jax.nn.silu(h @ wg) * (h @ wu)) @ wd)


def _in_proj(h, w_in, g_v):
    z = h @ w_in
    lead = z.shape[:-1]
    i1 = Q_W
    i2 = i1 + KV_W
    i3 = i2 + KV_W
    i4 = i3 + D_GMLP
    q = z[..., :i1].reshape(*lead, N_HEADS, HEAD_DIM)
    k = z[..., i1:i2].reshape(*lead, N_KV, HEAD_DIM)
    v = z[..., i2:i3].reshape(*lead, N_KV, HEAD_DIM)
    u = jax.nn.gelu(z[..., i3:i4])
    gv = _rmsnorm(jax.nn.gelu(z[..., i4:]), g_v).reshape(*lead, G_HEADS, G_DIM)
    return q, k, v, u, gv


def _sink_attend(q, k, v, valid, sinks):
    s = jnp.einsum('...qkgd,...skd->...kgqs', q, k).astype(jnp.float32) * (HEAD_DIM ** -0.5)
    s = jnp.where(valid, s, -jnp.inf)
    sink = jnp.broadcast_to(sinks.astype(jnp.float32).reshape(N_KV, GQA, 1, 1), s.shape[:-1] + (1,))
    p = jax.nn.softmax(jnp.concatenate([s, sink], axis=-1), axis=-1)[..., :-1]
    return jnp.einsum('...kgqs,...skd->...qkgd', p.astype(v.dtype), v)


def _swa_prompt(q, k, v, sinks):
    B, S = q.shape[:2]
    nb = S // WINDOW
    qb = q.reshape(B, nb, WINDOW, N_KV, GQA, HEAD_DIM)
    kp = jnp.pad(k, ((0, 0), (WINDOW, 0), (0, 0), (0, 0)))
    vp = jnp.pad(v, ((0, 0), (WINDOW, 0), (0, 0), (0, 0)))
    kb = jnp.concatenate([kp[:, :S].reshape(B, nb, WINDOW, N_KV, HEAD_DIM),
                          k.reshape(B, nb, WINDOW, N_KV, HEAD_DIM)], axis=2)
    vb = jnp.concatenate([vp[:, :S].reshape(B, nb, WINDOW, N_KV, HEAD_DIM),
                          v.reshape(B, nb, WINDOW, N_KV, HEAD_DIM)], axis=2)
    qi = jnp.arange(WINDOW)
    sj = jnp.arange(2 * WINDOW)
    n = jnp.arange(nb)
    dist = WINDOW + qi[:, None] - sj[None, :]
    kpos = (n[:, None] - 1) * WINDOW + sj[None, :]
    valid = ((dist >= 0) & (dist < WINDOW))[None] & (kpos >= 0)[:, None, :]
    o = _sink_attend(qb, kb, vb, valid[None, :, None, None], sinks)
    return o.reshape(B, S, Q_W)


def _swa_sample(q, k_all, v_all, kpos, qpos, sinks):
    Bd, T = q.shape[:2]
    qg = q.reshape(Bd, T, N_KV, GQA, HEAD_DIM)
    dist = qpos[:, None] - kpos[None, :]
    valid = (dist >= 0) & (dist < WINDOW)
    o = _sink_attend(qg, k_all, v_all, valid, sinks)
    return o.reshape(Bd, T, Q_W)


def _gmlp(u, gv, ws, b_s):
    B, L = u.shape[:2]
    c = min(L, CHUNK)
    vb = gv.reshape(B, L // c, c, G_HEADS, G_DIM)
    mixed = jnp.einsum('hts,bnshd->bnthd', ws[:, :c, :c], vb) + b_s[:, :c].T[None, None, :, :, None]
    return u * mixed.reshape(B, L, D_GMLP)


def _out_proj(ya, yg, g_a, g_g, w_out):
    return jnp.concatenate([_rmsnorm(ya, g_a), _rmsnorm(yg, g_g)], axis=-1) @ w_out


def setup_inputs(seed: int = 0) -> dict:
    key = jax.random.key(seed)
    ks = jax.random.split(key, 24)
    f32 = jnp.float32
    w_buf = min(WINDOW, PAST_LEN)

    def nrm(k, shape, scale):
        return jax.random.normal(k, shape, f32) * scale

    def gain(k, shape):
        return 1.0 + 0.02 * jax.random.normal(k, shape, f32)

    return {
        "x_prompt": nrm(ks[0], (BATCH, SEQ, D_MODEL), 1.0),
        "x_sample": nrm(ks[1], (DEC_BATCH, DEC_SEQ, D_MODEL), 1.0),
        "cache_k_win": nrm(ks[2], (DEPTH, DEC_BATCH, w_buf, N_KV, HEAD_DIM), 1.0),
        "cache_v_win": nrm(ks[3], (DEPTH, DEC_BATCH, w_buf, N_KV, HEAD_DIM), 1.0),
        "norm_ffn1": gain(ks[4], (DEPTH, D_MODEL)),
        "ffn1_gate": nrm(ks[5], (DEPTH, D_MODEL, D_FF), D_MODEL ** -0.5),
        "ffn1_up": nrm(ks[6], (DEPTH, D_MODEL, D_FF), D_MODEL ** -0.5),
        "ffn1_down": nrm(ks[7], (DEPTH, D_FF, D_MODEL), D_FF ** -0.5),
        "norm_mix": gain(ks[8], (DEPTH, D_MODEL)),
        "w_in": nrm(ks[9], (DEPTH, D_MODEL, D_IN), D_MODEL ** -0.5),
        "attn_sinks": nrm(ks[10], (DEPTH, N_HEADS), 0.5),
        "gmlp_v_norm": gain(ks[11], (DEPTH, D_GMLP)),
        "gmlp_w_s": nrm(ks[12], (DEPTH, G_HEADS, CHUNK, CHUNK), CHUNK ** -0.5),
        "gmlp_b_s": gain(ks[13], (DEPTH, G_HEADS, CHUNK)),
        "norm_attn_out": gain(ks[14], (DEPTH, D_ATTN)),
        "norm_gmlp_out": gain(ks[15], (DEPTH, D_GMLP)),
        "w_out": nrm(ks[16], (DEPTH, D_MIX, D_MODEL), D_MIX ** -0.5),
        "norm_ffn2": gain(ks[17], (DEPTH, D_MODEL)),
        "ffn2_gate": nrm(ks[18], (DEPTH, D_MODEL, D_FF), D_MODEL ** -0.5),
        "ffn2_up": nrm(ks[19], (DEPTH, D_MODEL, D_FF), D_MODEL ** -0.5),
        "ffn2_down": nrm(ks[20], (DEPTH, D_FF, D_MODEL), D_FF ** -0.5),
        "norm_final": gain(ks[21], (D_MODEL,)),
    }


def reference(x_prompt, x_sample, cache_k_win, cache_v_win, norm_ffn1, ffn1_gate, ffn1_up, ffn1_down,
              norm_mix, w_in, attn_sinks, gmlp_v_norm, gmlp_w_s, gmlp_b_s, norm_attn_out, norm_gmlp_out,
              w_out, norm_ffn2, ffn2_gate, ffn2_up, ffn2_down, norm_final):
    S = x_prompt.shape[1]
    T = x_sample.shape[1]
    w_buf = cache_k_win.shape[2]
    pos_p = jnp.arange(S, dtype=jnp.int32)
    pos_s = PAST_LEN + jnp.arange(T, dtype=jnp.int32)
    kpos_s = jnp.concatenate([PAST_LEN - w_buf + jnp.arange(w_buf, dtype=jnp.int32), pos_s])
    tril = jnp.tril(jnp.ones((CHUNK, CHUNK), dtype=bool))

    hp, hs = x_prompt, x_sample
    kwp, vwp, kws, vws, gvp, gvs = [], [], [], [], [], []
    for l in range(DEPTH):
        hp = _ffn_half(hp, norm_ffn1[l], ffn1_gate[l], ffn1_up[l], ffn1_down[l])
        hs = _ffn_half(hs, norm_ffn1[l], ffn1_gate[l], ffn1_up[l], ffn1_down[l])
        ws = jnp.where(tril, gmlp_w_s[l], jnp.zeros((), gmlp_w_s.dtype))

        q, k, v, u, gv = _in_proj(_rmsnorm(hp, norm_mix[l]), w_in[l], gmlp_v_norm[l])
        q = _rope(q, pos_p)
        k = _rope(k, pos_p)
        ya = _swa_prompt(q, k, v, attn_sinks[l])
        yg = _gmlp(u, gv, ws, gmlp_b_s[l])
        hp = hp + _out_proj(ya, yg, norm_attn_out[l], norm_gmlp_out[l], w_out[l])
        kwp.append(k[:, -min(WINDOW, S):])
        vwp.append(v[:, -min(WINDOW, S):])
        gvp.append(gv[:, -CHUNK:])

        q, k, v, u, gv = _in_proj(_rmsnorm(hs, norm_mix[l]), w_in[l], gmlp_v_norm[l])
        q = _rope(q, pos_s)
        k = _rope(k, pos_s)
        k_all = jnp.concatenate([cache_k_win[l].astype(k.dtype), k], axis=1)
        v_all = jnp.concatenate([cache_v_win[l].astype(v.dtype), v], axis=1)
        ya = _swa_sample(q, k_all, v_all, kpos_s, pos_s, attn_sinks[l])
        yg = _gmlp(u, gv, ws, gmlp_b_s[l])
        hs = hs + _out_proj(ya, yg, norm_attn_out[l], norm_gmlp_out[l], w_out[l])
        kws.append(k_all[:, -w_buf:])
        vws.append(v_all[:, -w_buf:])
        gvs.append(gv)

        hp = _ffn_half(hp, norm_ffn2[l], ffn2_gate[l], ffn2_up[l], ffn2_down[l])
        hs = _ffn_half(hs, norm_ffn2[l], ffn2_gate[l], ffn2_up[l], ffn2_down[l])

    y_prompt = _rmsnorm(hp, norm_final)
    y_sample = _rmsnorm(hs, norm_final)
    return (y_prompt, y_sample, jnp.stack(kwp), jnp.stack(vwp), jnp.stack(kws), jnp.stack(vws),
            jnp.stack(gvp), jnp.stack(gvs))
```

```python
import contextlib
import os
import numpy as np
import ml_dtypes
import concourse.bass as bass
import concourse.mybir as mybir
from concourse.bass_utils import run_bass_kernel_spmd

F32 = mybir.dt.float32
BF16 = mybir.dt.bfloat16
AF = mybir.ActivationFunctionType
ALU = mybir.AluOpType

P = 128
NT = 9
NPASS = 2
TOK = NT * P
TS = 384
NTS = TOK // TS
DM = 1024
KC = 8
DFF = 2816
NFC = 22
DIN = 1792
PAST_LEN = 16384
EPS = 1e-6
GELU_C1 = 1.5957691216057308
GELU_C0 = 0.7978845608028654
GELU_C2 = 0.044715
NTILES = NT * NPASS


class Buf:
    __slots__ = ("name", "w", "r", "excl", "strict")

    def __init__(self, name, excl=False, strict=False):
        self.name = name
        self.w = None
        self.r = []
        self.excl = excl
        self.strict = strict


class Sched:
    ENG = ("pe", "act", "dve", "pool", "sp")

    def __init__(self):
        self.q = {e: [] for e in self.ENG}
        self.cnt = {}
        self.seen = {e: {} for e in self.ENG}

    def _need(self, eng, waits, tok, raw):
        if tok is None:
            return
        k, v = tok
        if k == eng and not raw:
            return
        if self.seen[eng].get(k, 0) >= v:
            return
        if waits.get(k, 0) < v:
            waits[k] = v

    def _deps(self, eng, R, W, extra):
        waits = {}
        for b in R:
            self._need(eng, waits, b.w, True)
            if b.excl:
                for t in b.r:
                    self._need(eng, waits, t, False)
        for b in W:
            self._need(eng, waits, b.w, b.strict)
            for t in b.r:
                self._need(eng, waits, t, False)
        for t in extra:
            self._need(eng, waits, t, True)
        for k, v in waits.items():
            self.seen[eng][k] = v
        return list(waits.items())

    def _commit(self, tok, R, W):
        for b in R:
            b.r.append(tok)
        for b in W:
            b.w = tok
            b.r = []

    def op(self, eng, fn, R=(), W=(), extra=()):
        return self.group(eng, [fn], R, W, extra)

    def group(self, eng, fns, R=(), W=(), extra=()):
        waits = self._deps(eng, R, W, extra)
        n = self.cnt.get(eng, 0) + 1
        self.cnt[eng] = n
        tok = (eng, n)
        for i, fn in enumerate(fns):
            self.q[eng].append((fn, waits if i == 0 else [], (eng, 1) if i == len(fns) - 1 else None))
        self._commit(tok, R, W)
        return tok

    def dma(self, eng, fn, sem, R=(), W=(), extra=()):
        waits = self._deps(eng, R, W, extra)
        n = self.cnt.get(sem, 0) + 16
        self.cnt[sem] = n
        tok = (sem, n)
        self.q[eng].append((fn, waits, (sem, 16)))
        self._commit(tok, R, W)
        return tok

    def wait_only(self, eng, toks):
        waits = {}
        for t in toks:
            self._need(eng, waits, t, True)
        for k, v in waits.items():
            self.seen[eng][k] = v
        self.q[eng].append((None, list(waits.items()), None))

    def emit(self, eng, e, sems):
        for fn, waits, inc in self.q[eng]:
            for k, v in waits:
                e.wait_ge(sems[k], v)
            if fn is None:
                continue
            ins = fn(e)
            if inc is not None:
                ins.then_inc(sems[inc[0]], inc[1])


def build_program():
    nc = bass.Bass("TRN2", target_bir_lowering=False)
    S = Sched()
    es = contextlib.ExitStack()

    def din(name, shape, dt=F32):
        return nc.dram_tensor(name, list(shape), dt, kind="ExternalInput").ap()

    def dout(name, shape, dt=F32):
        return nc.dram_tensor(name, list(shape), dt, kind="ExternalOutput").ap()

    def sb(name, shape, dt):
        return es.enter_context(nc.sbuf_tensor(name, list(shape), dt))

    xin = din("xin", [NTILES, P, DM])
    rope = din("rope", [NTILES, P, 128])
    masks_d = din("masks", [P, 5 * 128], BF16)
    ck_d = din("ck", [16, P, 128])
    cv_d = din("cv", [16, P, 128])
    wgu_d = [din(f"wgu{f}", [NFC, P, 2048]) for f in range(2)]
    wd_d = [din(f"wd{f}", [KC, P, DFF]) for f in range(2)]
    win_d = din("win", [KC, P, DIN])
    wout_d = din("wout", [KC, P, DM])
    gfin_d = din("gfin", [P, DM])
    gfm_d = din("gfm", [P, 24])
    g512_d = din("g512", [3, P, 512])
    sinks_d = din("sinks", [P, 8])
    wsT_d = din("wsT", [P, 8 * 128])
    trilT_d = din("trilT", [P, 128])
    bsT_d = din("bsT", [P, 16])
    ws4_d = din("ws4", [4, 32])
    tril4_d = din("tril4", [4, 32])
    rep_d = din("rep", [4, 64], BF16)
    dmask_d = din("dmask", [64, 64], BF16)

    yout = dout("yout", [NTILES - 1, P, DM])
    kwp_o = dout("kwp", [P, 128])
    vwp_o = dout("vwp", [P, 128])
    gvp_o = dout("gvp", [P, 512])
    kws_o = dout("kws", [16, P, 128])
    vws_o = dout("vws", [16, P, 128])
    gvs_o = dout("gvs", [64, 512])

    xres = sb("xres", [P, NT, DM], F32)
    hT = sb("hT", [P, KC, TOK], BF16)
    hid = sb("hid", [P, NFC * TOK], BF16)
    gus = [sb(f"gus{i}", [P, 2048], BF16) for i in range(2)]
    wds = [sb(f"wds{i}", [P, DFF], BF16) for i in range(2)]
    xs = [sb(f"xs{i}", [P, DM], BF16) for i in range(2)]
    junk = sb("junk", [P, DM], BF16)
    mhalf = sb("mhalf", [P, 1], F32)
    silu_t = [sb(f"silu{i}", [P, TS], F32) for i in range(2)]
    dn_t = [sb(f"dn{i}", [P, TS], F32) for i in range(2)]
    stat = sb("stat", [P, 32], F32)
    gfin = sb("gfins", [P, DM], F32)
    gfm = sb("gfms", [P, 3, 8], F32)
    g512 = sb("g512s", [P, 3, 512], F32)
    gl1 = sb("gl1", [P, 1024], F32)
    wsTr = gl1
    esink = sb("esink", [P, 8], F32)
    wsT = sb("wsTb", [P, 8, 128], BF16)
    trilT = sb("trilTs", [P, 128], F32)
    bsT = sb("bsTs", [P, 16], F32)
    ws4r = sb("ws4r", [4, 32], F32)
    tril4 = sb("tril4s", [4, 32], F32)
    ws4b = sb("ws4b", [4, 8, 4], BF16)
    rep = sb("reps", [4, 64], BF16)
    dmask = sb("dmasks", [64, 64], BF16)
    bsamp = sb("bsamp", [4, 8 * 64], BF16)
    wsamp = sb("wsamp", [64, 8, 64], BF16)
    masks = sb("maskss", [P, 5, 128], BF16)
    identb = sb("identb", [P, 128], BF16)
    identf = sb("identf", [P, 128], F32)
    ropet = [sb(f"ropet{i}", [P, 128], F32) for i in range(2)]
    qkr = sb("qkr", [P, 768], BF16)
    kro = sb("kro", [P, 128], F32)
    vro = sb("vro", [P, 128], F32)
    vaug = [sb(f"vaug{i}", [P, 2, 65], BF16) for i in range(3)]
    kT = [sb(f"kT{i}", [P, 2, 128], BF16) for i in range(3)]
    qT = [sb(f"qT{i}", [P, 4, 128], BF16) for i in range(2)]
    PT = sb("PT", [P, 4, 512], BF16)
    ya = sb("ya", [P, 512], F32)
    yan = sb("yan", [P, 512], BF16)
    ugv = [sb(f"ugv{i}", [P, 1024], F32) for i in range(2)]
    ropeR = sb("ropeR", [P, 640], F32)
    gvn = sb("gvn", [P, 512], F32)
    gvnb = sb("gvnb", [P, 512], BF16)
    yg = sb("yg", [P, 512], F32)
    ygn = sb("ygn", [P, 512], BF16)
    cT = sb("cT", [P, 8, 128], BF16)
    kTc = sb("kTc", [P, 16, 2, 128], BF16)
    vaugc = sb("vaugc", [P, 16, 2, 65], BF16)
    oTs = ya
    ps = es.enter_context(nc.psum_tensor("ps", [P, 8, 512], F32))

    def bank(i):
        return ps[:, i, :]

    def bank_bf(i):
        return ps[:, i, :].bitcast(BF16)

    B = lambda n: Buf(n)
    b_x = [B(f"x{i}") for i in range(NT)]
    b_hT = [B(f"hT{i}") for i in range(NT)]
    b_hid = [B(f"hid{i}") for i in range(NFC)]
    b_win = b_hid[0:13]
    b_wout = b_hid[12:20]
    b_gus = [B("gus0"), B("gus1")]
    b_wds = [B("wds0"), B("wds1")]
    b_xs = [B("xs0"), B("xs1")]
    b_junk = Buf("junk", strict=True)
    b_j2 = Buf("j2", strict=True)
    b_silu = [B("silu0"), B("silu1")]
    b_dn = [B("dn0"), B("dn1")]
    b_stat = [B(f"stat{i}") for i in range(32)]
    b_const = B("const")
    b_ropet = [B("ropet0"), B("ropet1")]
    b_qkr, b_kro, b_vro = B("qkr"), B("kro"), B("vro")
    b_vaug = [B("vaug0"), B("vaug1"), B("vaug2")]
    b_kT = [B("kT0"), B("kT1"), B("kT2")]
    b_ya, b_yan, b_gl1 = B("ya"), B("yan"), B("gl1")
    b_qT = [B("qT0"), B("qT1")]
    b_ugv = [B("ugv0"), B("ugv1")]
    b_PT = [B(f"PT{i}") for i in range(4)]
    b_gvn, b_gvnb, b_yg, b_ygn, b_cT = B("gvn"), B("gvnb"), B("yg"), B("ygn"), B("cT")
    b_kTc, b_vaugc = B("kTc"), B("vaugc")
    b_oTs = b_ya
    b_ropeR = B("ropeR")
    b_ps = [Buf(f"ps{i}", excl=True) for i in range(8)]
    out_toks = []

    b_c2 = B("const2")
    b_mh = B("mhalf")
    S.op("dve", lambda e: e.memset(mhalf[:, :], -0.5), W=[b_mh])
    cache_views = {}

    def emit_const_loads():
        def load(eng, dst, src, sem, W, R=()):
            if sem == "c0":
                tok = S.dma(eng, lambda e: e.dma_start(out=dst, in_=src), sem)
                b_const.w = tok
                return tok
            return S.dma(eng, lambda e: e.dma_start(out=dst, in_=src), sem, R=R, W=W)

        load("sp", gfin[:, :], gfin_d[:, :], "c0", [b_const])
        load("sp", gfm[:, :, :], gfm_d.rearrange("p (g k) -> p g k", g=3), "c0", [b_const])
        load("sp", g512[:, :, :], g512_d.rearrange("g p d -> p g d"), "c0", [b_const])
        load("sp", esink[:, :], sinks_d[:, :], "c0", [b_const])
        load("sp", wsTr[:, :], wsT_d[:, :], "c0", [b_const])
        load("sp", trilT[:, :], trilT_d[:, :], "c0", [b_const])
        load("sp", bsT[:, :], bsT_d[:, :], "c0", [b_const])
        load("sp", ws4r[:, :], ws4_d[:, :], "c0", [b_const])
        load("sp", tril4[:, :], tril4_d[:, :], "c0", [b_const])
        load("sp", rep[:, :], rep_d[:, :], "c0", [b_const])
        load("sp", dmask[:, :], dmask_d[:, :], "c0", [b_const])
        load("sp", masks[:, :, :], masks_d.rearrange("p (m q) -> p m q", m=5), "c0", [b_const])

    def emit_setup():
        S.op("dve", lambda e: e.tensor_copy(out=identb[:, :], in_=masks[:, 4, :]), R=[b_const], W=[b_c2])
        S.op("dve", lambda e: e.tensor_copy(out=identf[:, :], in_=masks[:, 4, :]), R=[b_const], W=[b_c2])
        S.op("act", lambda e: e.activation(out=esink[:, :], in_=esink[:, :], func=AF.Exp), R=[b_const], W=[b_c2])
        S.op("dve", lambda e: e.tensor_tensor(
            out=wsT[:, :, :], in0=wsTr[:, :].rearrange("p (h t) -> p h t", h=8),
            in1=trilT[:, :].unsqueeze(1).broadcast_to([P, 8, 128]), op=ALU.mult), R=[b_const, b_gl1], W=[b_c2])
        S.op("dve", lambda e: e.tensor_tensor(
            out=ws4b[:, :, :], in0=ws4r[:, :].rearrange("p (h s) -> p h s", h=8),
            in1=tril4[:, :].rearrange("p (h s) -> p h s", h=8), op=ALU.mult), R=[b_const], W=[b_c2])
        for i in range(3):
            S.op("dve", (lambda i: lambda e: e.memset(vaug[i][:, :, 64:65], 1.0))(i), W=[b_vaug[i]])
        S.op("dve", lambda e: e.memset(vaugc[:, :, :, 64:65], 1.0), W=[b_vaugc])

    def emit_setup_b():
        S.group("pe", [(lambda h: lambda e: e.matmul(
            ps[0:4, 6, h * 64:(h + 1) * 64], lhsT=ws4b[:, h, :], rhs=rep[:, :], start=True, stop=True))(h)
            for h in range(8)], R=[b_c2, b_const], W=[b_ps[6]])
        S.op("act", lambda e: e.activation(out=bsamp[:, :], in_=ps[0:4, 6, :], func=AF.Copy), R=[b_ps[6]], W=[b_c2])
        S.op("pe", lambda e: e.matmul(ps[0:64, 7, :], lhsT=rep[:, :], rhs=bsamp[:, :], start=True, stop=True),
             R=[b_c2, b_const], W=[b_ps[7]])
        S.op("dve", lambda e: e.tensor_tensor(
            out=wsamp[:, :, :], in0=ps[0:64, 7, :].rearrange("p (h t) -> p h t", h=8),
            in1=dmask[:, :].unsqueeze(1).broadcast_to([64, 8, 64]), op=ALU.mult), R=[b_ps[7], b_const], W=[b_c2])


    def emit_cache_prep(c):
        kst, vst = gl1, ugv[0]
        kb = ugv[1][:, :].bitcast(BF16)

        def dmas(hf):
            S.dma("sp", lambda e: e.dma_start(
                out=kst[:, :].rearrange("p (b d) -> p b d", b=8),
                in_=ck_d[hf * 8:(hf + 1) * 8, :, :].rearrange("b s d -> s b d")), "c1", W=[b_gl1])
            S.dma("sp", lambda e: e.dma_start(
                out=vst[:, :].rearrange("p (b d) -> p b d", b=8),
                in_=cv_d[hf * 8:(hf + 1) * 8, :, :].rearrange("b s d -> s b d")), "c2", W=[b_ugv[0]])

        def compute(hf):
            S.op("dve", lambda e: e.tensor_copy(
                out=kb.rearrange("p (bg u d) -> p bg u d", u=2, d=64),
                in_=kst[:, :].rearrange("p (bg d) -> p bg d", d=64).unsqueeze(2).broadcast_to([P, 16, 2, 64])),
                R=[b_gl1], W=[b_ugv[1]])
            S.op("act", lambda e: e.activation(
                out=vaugc[:, hf * 8:(hf + 1) * 8, :, 0:64],
                in_=vst[:, :].rearrange("p (b g d) -> p b g d", b=8, g=2), func=AF.Copy),
                R=[b_ugv[0]], W=[b_vaugc])
            for q2 in range(2):
                bk = 6 + q2
                S.group("pe", [(lambda j, bk: lambda e: e.transpose(
                    bank_bf(bk)[:, (j % 8) * 128:(j % 8 + 1) * 128], kb[:, j * 128:(j + 1) * 128],
                    identb[:, :]))(j, bk) for j in range(q2 * 8, q2 * 8 + 8)], R=[b_ugv[1], b_c2], W=[b_ps[bk]])
                S.op("act", (lambda q2, bk: lambda e: e.activation(
                    out=kTc[:, hf * 8 + q2 * 4:hf * 8 + (q2 + 1) * 4, :, :],
                    in_=bank_bf(bk).rearrange("p (b g s) -> p b g s", b=4, g=2), func=AF.Copy))(q2, bk),
                    R=[b_ps[bk]], W=[b_kTc])

        if c == 0:
            emit_setup_b()
        elif c == 1:
            dmas(0)
        elif c == 4:
            compute(0)
            dmas(1)
        elif c == 7:
            compute(1)

    def rms_stats(src_ap, src_bufs, width, slot, eps_mul=1.0, on_dve=False):
        if on_dve:
            j2 = ropeR[:, 0:512].bitcast(BF16)
            S.op("dve", lambda e: e.scalar_tensor_tensor(out=j2[:, 0:width], in0=src_ap, scalar=1.0, in1=src_ap,
                                                         op0=ALU.mult, op1=ALU.mult,
                                                         accum_out=stat[:, slot:slot + 1]),
                 R=src_bufs, W=[b_stat[slot], b_ropeR, b_j2])
        else:
            S.op("act", lambda e: e.activation(out=junk[:, 0:width], in_=src_ap, func=AF.Square,
                                               accum_out=stat[:, slot:slot + 1]),
                 R=src_bufs, W=[b_stat[slot], b_junk])
        S.op("pool", lambda e: e.tensor_scalar(out=stat[:, slot:slot + 1], in0=stat[:, slot:slot + 1],
                                               scalar1=1.0 / width, scalar2=EPS * eps_mul,
                                               op0=ALU.mult, op1=ALU.add),
             R=[b_stat[slot]], W=[b_stat[slot]])
        S.op("pool", lambda e: e.tensor_tensor(out=stat[:, slot:slot + 1], in0=stat[:, slot:slot + 1],
                                               in1=mhalf[:, 0:1], op=ALU.pow),
             R=[b_stat[slot], b_mh], W=[b_stat[slot]])

    def norm_stats_all(tiles=None, use_dve=True):
        for i, t in enumerate(range(NT) if tiles is None else tiles):
            rms_stats(xres[:, t, :], [b_x[t]], DM, 16 + t, on_dve=(use_dve and i % 3 == 2))

    def norm_apply_batch(tiles, gi):
        n = len(tiles)

        def ts(i):
            t = tiles[i]
            S.op("dve", lambda e: e.tensor_scalar(out=xs[i % 2][:, :], in0=xres[:, t, :],
                                                  scalar1=stat[:, 16 + t:17 + t], scalar2=None, op0=ALU.mult),
                 R=[b_x[t], b_stat[16 + t]], W=[b_xs[i % 2]])

        def tr(i):
            bk = 6 + i % 2
            S.group("pe", [(lambda kc: lambda e: e.transpose(
                bank_bf(bk)[:, kc * 128:(kc + 1) * 128], xs[i % 2][:, kc * 128:(kc + 1) * 128], identb[:, :]))(kc)
                for kc in range(KC)], R=[b_xs[i % 2], b_c2], W=[b_ps[bk]])

        def mu(i):
            t = tiles[i]
            bk = 6 + i % 2
            S.op("dve", lambda e: e.tensor_tensor(
                out=hT[:, :, t * P:(t + 1) * P], in0=bank_bf(bk).rearrange("p (k q) -> p k q", k=KC),
                in1=gfm[:, gi, :].unsqueeze(2).broadcast_to([P, KC, 128]), op=ALU.mult),
                R=[b_ps[bk], b_const], W=[b_hT[t]])

        for i in range(n + 2):
            if i < n:
                ts(i)
            if 1 <= i <= n:
                tr(i - 1)
            if 2 <= i <= n + 1:
                mu(i - 2)

    def norm_single(t, gi, bk):
        rms_stats(xres[:, t, :], [b_x[t]], DM, 16 + t)
        S.op("dve", lambda e: e.tensor_scalar(out=xs[0][:, :], in0=xres[:, t, :],
                                              scalar1=stat[:, 16 + t:17 + t], scalar2=None, op0=ALU.mult),
             R=[b_x[t], b_stat[16 + t]], W=[b_xs[0]])
        S.group("pe", [(lambda kc: lambda e: e.transpose(
            bank_bf(bk)[:, kc * 128:(kc + 1) * 128], xs[0][:, kc * 128:(kc + 1) * 128], identb[:, :]))(kc)
            for kc in range(KC)], R=[b_xs[0], b_c2], W=[b_ps[bk]])
        S.op("dve", lambda e: e.tensor_tensor(
            out=hT[:, :, t * P:(t + 1) * P], in0=bank_bf(bk).rearrange("p (k q) -> p k q", k=KC),
            in1=gfm[:, gi, :].unsqueeze(2).broadcast_to([P, KC, 128]), op=ALU.mult),
            R=[b_ps[bk], b_const], W=[b_hT[t]])

    def gu_dma(f, c):
        sl = c % 2
        S.dma("pool", lambda e: e.dma_start(out=gus[sl][:, :], in_=wgu_d[f][c, :, :]), f"gu{sl}", W=[b_gus[sl]])

    def wd_dma(f, oc):
        sl = oc % 2
        S.dma("pool", lambda e: e.dma_start(out=wds[sl][:, :], in_=wd_d[f][oc, :, :]), f"wd{sl}", W=[b_wds[sl]])

    APPLY_MODE = os.environ.get("MK_APPLY", "all")

    def ffn(f, prefetched, segs, norms_done=False, stats_done=False, mid_hook=None):
        gi = 0 if f == 0 else 2
        if not prefetched:
            gu_dma(f, 0)
            gu_dma(f, 1)
        seg_tiles = [list(range(st // P, (st + w + P - 1) // P)) for st, w in segs]
        if not norms_done:
            all_tiles = [t for tl in seg_tiles for t in tl]
            if not stats_done:
                norm_stats_all(all_tiles)
            if APPLY_MODE == "all":
                norm_apply_batch(all_tiles, gi)
            elif APPLY_MODE == "2+1":
                norm_apply_batch(seg_tiles[0] + seg_tiles[1], gi)
            else:
                norm_apply_batch(seg_tiles[0], gi)
        wd_dma(f, 0)
        wd_dma(f, 1)
        it = 0
        for c in range(NFC):
            sl = c % 2
            for si, (st, w) in enumerate(segs):
                if c == 0 and not norms_done:
                    if APPLY_MODE == "2+1" and si == 1:
                        norm_apply_batch(seg_tiles[2], gi)
                    elif APPLY_MODE == "seg" and si >= 1:
                        norm_apply_batch(seg_tiles[si], gi)
                g = it % 2
                it += 1
                fns = []
                for which, bk in ((0, g), (1, 2 + g)):
                    for kc in range(KC):
                        fns.append((lambda which, bk, kc, sl, st, w: lambda e: e.matmul(
                            ps[:, bk, 0:w], lhsT=gus[sl][:, which * 1024 + kc * 128: which * 1024 + (kc + 1) * 128],
                            rhs=hT[:, kc, st:st + w], start=(kc == 0), stop=(kc == KC - 1)))(which, bk, kc, sl, st, w))
                S.group("pe", fns, R=[b_gus[sl]] + [b_hT[t] for t in seg_tiles[si]], W=[b_ps[g], b_ps[2 + g]])
                S.op("act", (lambda g, w: lambda e: e.activation(out=silu_t[g][:, 0:w], in_=ps[:, g, 0:w],
                                                                 func=AF.Silu))(g, w),
                     R=[b_ps[g]], W=[b_silu[g]])
                S.op("dve", (lambda g, c, st, w: lambda e: e.tensor_tensor(
                    out=hid[:, c * TOK + st: c * TOK + st + w], in0=ps[:, 2 + g, 0:w], in1=silu_t[g][:, 0:w],
                    op=ALU.mult))(g, c, st, w), R=[b_ps[2 + g], b_silu[g]], W=[b_hid[c]])
            if c + 2 < NFC:
                gu_dma(f, c + 2)
            if mid_hook is not None:
                mid_hook(c)
        pend = None
        it = 0

        def finish(pd):
            g, oc, si = pd
            st, w = segs[si]
            tl = seg_tiles[si]
            nfull = w // P
            rem = w - nfull * P
            fns = [(lambda j, g: lambda e: e.transpose(
                ps[:, 6 + g, j * 128:(j + 1) * 128], dn_t[g][:, j * 128:(j + 1) * 128], identf[:, :]))(j, g)
                for j in range(nfull)]
            if rem:
                fns.append((lambda g: lambda e: e.transpose(
                    ps[:, 6 + g, nfull * 128:(nfull + 1) * 128], dn_t[g][:, nfull * 128:(nfull + 1) * 128],
                    identf[:, :]))(g))
            S.group("pe", fns, R=[b_dn[g], b_c2], W=[b_ps[6 + g]])
            t0 = tl[0]
            S.op("dve", (lambda g, oc: lambda e: e.scalar_tensor_tensor(
                out=xres[:, t0:t0 + nfull, oc * 128:(oc + 1) * 128],
                in0=ps[:, 6 + g, 0:nfull * 128].rearrange("p (j q) -> p j q", j=nfull), scalar=0.5,
                in1=xres[:, t0:t0 + nfull, oc * 128:(oc + 1) * 128], op0=ALU.mult, op1=ALU.add))(g, oc),
                R=[b_ps[6 + g]] + [b_x[t] for t in tl], W=[b_x[t] for t in tl])
            if rem:
                tr = t0 + nfull
                S.op("dve", (lambda g, oc: lambda e: e.scalar_tensor_tensor(
                    out=xres[0:rem, tr, oc * 128:(oc + 1) * 128],
                    in0=ps[0:rem, 6 + g, nfull * 128:(nfull + 1) * 128], scalar=0.5,
                    in1=xres[0:rem, tr, oc * 128:(oc + 1) * 128], op0=ALU.mult, op1=ALU.add))(g, oc),
                    R=[b_ps[6 + g], b_x[tr]], W=[b_x[tr]])

        for oc in range(KC):
            sl = oc % 2
            for si, (st, w) in enumerate(segs):
                g = it % 2
                it += 1
                S.group("pe", [(lambda c, g, sl, st, w: lambda e: e.matmul(
                    ps[:, 4 + g, 0:w], lhsT=wds[sl][:, c * 128:(c + 1) * 128],
                    rhs=hid[:, c * TOK + st: c * TOK + st + w],
                    start=(c == 0), stop=(c == NFC - 1)))(c, g, sl, st, w) for c in range(NFC)],
                    R=[b_wds[sl]] + b_hid, W=[b_ps[4 + g]])
                S.op("act", (lambda g, w: lambda e: e.activation(out=dn_t[g][:, 0:w], in_=ps[:, 4 + g, 0:w],
                                                                 func=AF.Copy))(g, w),
                     R=[b_ps[4 + g]], W=[b_dn[g]])
                if pend is not None:
                    finish(pend)
                pend = (g, oc, si)
            if oc + 2 < KC:
                wd_dma(f, oc + 2)
        finish(pend)

    b_winB = B("winB")

    def load_weights_mix(which):
        if which == "winA":
            tok = S.dma("pool", lambda e: e.dma_start(out=hid[:, 0:768], in_=win_d[0, :, 0:768]), "winA", W=b_win)
            for kc in range(1, KC):
                tok = S.dma("pool", (lambda kc: lambda e: e.dma_start(
                    out=hid[:, kc * DIN:kc * DIN + 768], in_=win_d[kc, :, 0:768]))(kc), "winA")
            for b in b_win:
                b.w = tok
                b.r = []
            return
        if which == "winB":
            for kc in range(KC):
                tokb = S.dma("pool", (lambda kc: lambda e: e.dma_start(
                    out=hid[:, kc * DIN + 768:(kc + 1) * DIN], in_=win_d[kc, :, 768:DIN]))(kc), "winB")
            b_winB.w = tokb
            b_winB.r = []
            return
        name, bufs = "wout", b_wout
        fn = lambda kc: (lambda e: e.dma_start(out=hid[:, 14336 + kc * DM: 14336 + (kc + 1) * DM],
                                               in_=wout_d[kc, :, :]))
        tok = S.dma("pool", fn(0), name, W=bufs)
        for kc in range(1, KC):
            tok = S.dma("pool", fn(kc), name)
        for b in bufs:
            b.w = tok
            b.r = []

    def mix_front(t, gt):
        halo = gt == 0
        sample = gt == NTILES - 1
        cur = gt % 3
        rs = gt % 2
        ub = gt % 2
        UG, b_UG = ugv[ub], b_ugv[ub]
        QT, b_QT = qT[ub], b_qT[ub]
        ropeA, b_ropeA = UG, b_UG
        S.dma("sp", lambda e: e.dma_start(out=ropet[rs][:, :], in_=rope[gt, :, :]), f"rp{rs}", W=[b_ropet[rs]])
        widths = (512, 256, 512, 512)
        offs = (0, 512, 768, 1280)

        def in_proj(nb):
            S.group("pe", [(lambda kc: lambda e: e.matmul(
                ps[:, nb, 0:widths[nb]], lhsT=hT[:, kc, t * P:(t + 1) * P],
                rhs=hid[:, kc * DIN + offs[nb]: kc * DIN + offs[nb] + widths[nb]],
                start=(kc == 0), stop=(kc == KC - 1)))(kc) for kc in range(KC)],
                R=[b_hT[t]] + b_win + ([b_winB] if nb >= 2 else []), W=[b_ps[nb]])

        in_proj(0)
        yield
        in_proj(1)
        yield
        if gt >= NTILES - 2:
            S.op("act", lambda e: e.activation(out=vro[:, :], in_=ps[:, 1, 128:256], func=AF.Copy),
                 R=[b_ps[1]], W=[b_vro])
        S.op("act", lambda e: e.activation(out=vaug[cur][:, :, 0:64],
                                           in_=ps[:, 1, 128:256].rearrange("p (g d) -> p g d", g=2), func=AF.Copy),
             R=[b_ps[1]], W=[b_vaug[cur]])
        yield
        zq = ps[:, 0:2, :].rearrange("p a b -> p (a b)")[:, 0:640]
        z4 = zq.rearrange("p (h u d) -> p h u d", u=2, d=32)
        c2 = ropet[rs][:, 0:64].unsqueeze(1).broadcast_to([P, 10, 64])
        S.op("dve", lambda e: e.tensor_tensor(out=ropeA[:, 0:640].rearrange("p (h d) -> p h d", d=64),
                                              in0=zq.rearrange("p (h d) -> p h d", d=64), in1=c2, op=ALU.mult),
             R=[b_ps[0], b_ps[1], b_ropet[rs]], W=[b_ropeA])
        yield
        rb4 = ropeR[:, 0:640].rearrange("p (h u d) -> p h u d", u=2, d=32)
        S.op("dve", lambda e: e.tensor_tensor(
            out=rb4[:, :, 0, :], in0=z4[:, :, 1, :],
            in1=ropet[rs][:, 64:96].unsqueeze(1).broadcast_to([P, 10, 32]), op=ALU.mult),
            R=[b_ps[0], b_ps[1], b_ropet[rs]], W=[b_ropeR])
        yield
        S.op("dve", lambda e: e.tensor_tensor(
            out=rb4[:, :, 1, :], in0=z4[:, :, 0, :],
            in1=ropet[rs][:, 96:128].unsqueeze(1).broadcast_to([P, 10, 32]), op=ALU.mult),
            R=[b_ps[0], b_ps[1], b_ropet[rs]], W=[b_ropeR])
        yield
        S.op("pool", lambda e: e.tensor_tensor(out=qkr[:, 0:512], in0=ropeA[:, 0:512], in1=ropeR[:, 0:512], op=ALU.add),
             R=[b_ropeA, b_ropeR], W=[b_qkr])
        yield
        in_proj(2)
        yield
        in_proj(3)
        yield
        if gt >= NTILES - 2:
            S.op("pool", lambda e: e.tensor_tensor(out=kro[:, :], in0=ropeA[:, 512:640], in1=ropeR[:, 512:640],
                                                   op=ALU.add), R=[b_ropeA, b_ropeR], W=[b_kro])
            yield
            S.op("pool", lambda e: e.tensor_copy(
                out=qkr[:, 512:768].rearrange("p (g u d) -> p g u d", g=2, u=2),
                in_=kro[:, :].rearrange("p (g d) -> p g d", g=2).unsqueeze(2).broadcast_to([P, 2, 2, 64])),
                R=[b_kro], W=[b_qkr])
            yield
        else:
            for u in range(2):
                S.op("pool", (lambda u: lambda e: e.tensor_tensor(
                    out=qkr[:, 512:768].rearrange("p (g u d) -> p g u d", g=2, u=2)[:, :, u, :],
                    in0=ropeA[:, 512:640].rearrange("p (g d) -> p g d", g=2),
                    in1=ropeR[:, 512:640].rearrange("p (g d) -> p g d", g=2), op=ALU.add))(u),
                    R=[b_ropeA, b_ropeR], W=[b_qkr])
                yield
        S.group("pe", [(lambda j: lambda e: e.transpose(
            bank_bf(4)[:, j * 128:(j + 1) * 128], qkr[:, j * 128:(j + 1) * 128], identb[:, :]))(j)
            for j in (range(4, 6) if halo else range(6))], R=[b_qkr, b_c2], W=[b_ps[4]])
        yield
        S.op("act", lambda e: e.activation(out=kT[cur][:, :, :],
                                           in_=bank_bf(4)[:, 512:768].rearrange("p (g s) -> p g s", g=2), func=AF.Copy),
             R=[b_ps[4]], W=[b_kT[cur]])
        if gt == NTILES - 2:
            out_toks.append(S.dma("sp", lambda e: e.dma_start(out=kwp_o[:, :], in_=kro[:, :]), "o_k", R=[b_kro]))
            out_toks.append(S.dma("sp", lambda e: e.dma_start(out=vwp_o[:, :], in_=vro[:, :]), "o_v", R=[b_vro]))
        if sample:
            for tt in range(4):
                out_toks.append(S.dma("sp", (lambda tt: lambda e: e.dma_start(
                    out=kws_o[:, 124 + tt, :], in_=kro[tt * 16:(tt + 1) * 16, :]))(tt), "o_k", R=[b_kro]))
                out_toks.append(S.dma("sp", (lambda tt: lambda e: e.dma_start(
                    out=vws_o[:, 124 + tt, :], in_=vro[tt * 16:(tt + 1) * 16, :]))(tt), "o_v", R=[b_vro]))
        if halo:
            return
        S.op("act", lambda e: e.activation(out=QT[:, :, :],
                                           in_=bank_bf(4)[:, 0:512].rearrange("p (j q) -> p j q", j=4), func=AF.Copy),
             R=[b_ps[4]], W=[b_QT])
        yield
        S.op("act", lambda e: e.activation(out=gl1[:, :].rearrange("p (a b) -> p a b", a=2), in_=ps[:, 2:4, :],
                                           func=AF.Square), R=[b_ps[2], b_ps[3]], W=[b_gl1])
        yield
        S.op("pool", lambda e: e.tensor_scalar(out=gl1[:, :], in0=gl1[:, :], scalar1=GELU_C2, scalar2=1.0,
                                               op0=ALU.mult, op1=ALU.add), R=[b_gl1], W=[b_gl1])
        yield
        g2 = gl1[:, :].rearrange("p (a b) -> p a b", a=2)
        S.op("dve", lambda e: e.tensor_tensor(out=g2, in0=ps[:, 2:4, :], in1=g2, op=ALU.mult),
             R=[b_gl1, b_ps[2], b_ps[3]], W=[b_gl1])
        yield
        S.op("act", lambda e: e.activation(out=gl1[:, :], in_=gl1[:, :], func=AF.Tanh, scale=GELU_C0),
             R=[b_gl1], W=[b_gl1])
        yield
        for a in range(2):
            S.op("dve", (lambda a: lambda e: e.scalar_tensor_tensor(
                out=UG[:, a * 512:(a + 1) * 512], in0=gl1[:, a * 512:(a + 1) * 512], scalar=1.0,
                in1=ps[:, 2 + a, :], op0=ALU.add, op1=ALU.mult))(a), R=[b_gl1, b_ps[2 + a]], W=[b_UG])
            yield

    def mix_back(t, gt):
        sample = gt == NTILES - 1
        cur = gt % 3
        prv = (gt - 1) % 3
        ub = gt % 2
        UG, b_UG = ugv[ub], b_ugv[ub]
        QT, b_QT = qT[ub], b_qT[ub]
        if not sample:
            for blk in range(2):
                ks = prv if blk == 0 else cur
                for par in range(2):
                    bk = 5 + par
                    pi = blk * 2 + par
                    S.group("pe", [(lambda g, pp, par, bk, ks: lambda e: e.matmul(
                        ps[:, bk, (g * 2 + pp) * 128:(g * 2 + pp + 1) * 128],
                        lhsT=kT[ks][par * 64:par * 64 + 64, g, :],
                        rhs=QT[par * 64:par * 64 + 64, 2 * g + pp, :],
                        start=True, stop=True, skip_group_check=True))(g, pp, par, bk, ks)
                        for g in range(2) for pp in range(2)],
                        R=[b_kT[ks], b_QT], W=[b_ps[bk]])
                    yield
                    S.op("act", (lambda bk, pi: lambda e: e.activation(out=PT[:, pi, :], in_=ps[:, bk, :], func=AF.Exp,
                                                                       scale=0.125))(bk, pi), R=[b_ps[bk]], W=[b_PT[pi]])
                    yield
                    mi = (2 if gt == 1 else 0) if blk == 0 else 1
                    S.op("pool", (lambda pi, mi: lambda e: e.tensor_tensor(
                        out=PT[:, pi, :].rearrange("p (h q) -> p h q", h=4),
                        in0=PT[:, pi, :].rearrange("p (h q) -> p h q", h=4),
                        in1=masks[:, mi, :].unsqueeze(1).broadcast_to([P, 4, 128]), op=ALU.mult))(pi, mi),
                        R=[b_PT[pi], b_const], W=[b_PT[pi]])
                    yield
            for g in range(2):
                fns = []
                for hh in range(4):
                    par, pp = hh % 2, hh // 2
                    for blk in range(2):
                        ks = prv if blk == 0 else cur
                        fns.append((lambda hh, blk, ks, g, par, pp: lambda e: e.matmul(
                            ps[:, 5 + g, hh * 65:(hh + 1) * 65],
                            lhsT=PT[:, blk * 2 + par, (g * 2 + pp) * 128:(g * 2 + pp + 1) * 128],
                            rhs=vaug[ks][:, g, :], start=(blk == 0), stop=(blk == 1),
                            skip_group_check=True))(hh, blk, ks, g, par, pp))
                S.group("pe", fns, R=b_PT + [b_vaug[prv], b_vaug[cur]], W=[b_ps[5 + g]])
                yield
        else:
            fns = []
            for b in range(16):
                for h in range(8):
                    par, hq = h % 2, h // 2
                    fns.append((lambda b, h, par, hq: lambda e: e.matmul(
                        ps[:, par, hq * 64 + b:hq * 64 + 64:16],
                        lhsT=kTc[par * 64:par * 64 + 64, b, h // 4, :],
                        rhs=QT[par * 64:par * 64 + 64, hq, b:64:16],
                        start=True, stop=True, skip_group_check=True))(b, h, par, hq))
            S.group("pe", fns, R=[b_kTc, b_QT], W=[b_ps[0], b_ps[1]])
            yield
            S.group("pe", [(lambda h: lambda e: e.matmul(
                ps[0:64, 2 + h % 2, (h // 2) * 64:(h // 2 + 1) * 64],
                lhsT=kT[cur][(h % 2) * 64:(h % 2) * 64 + 64, h // 4, 0:64],
                rhs=QT[(h % 2) * 64:(h % 2) * 64 + 64, h // 2, 0:64], start=True, stop=True,
                skip_group_check=True))(h) for h in range(8)], R=[b_kT[cur], b_QT], W=[b_ps[2], b_ps[3]])
            yield
            for par in range(2):
                S.op("act", (lambda par: lambda e: e.activation(out=PT[:, par, 0:256], in_=ps[:, par, 0:256],
                                                                func=AF.Exp, scale=0.125))(par),
                     R=[b_ps[par]], W=[b_PT[par]])
                S.op("dve", (lambda par: lambda e: e.tensor_tensor(
                    out=PT[:, par, 0:256].rearrange("p (h q) -> p h q", h=4),
                    in0=PT[:, par, 0:256].rearrange("p (h q) -> p h q", h=4),
                    in1=masks[:, 3, 0:64].unsqueeze(1).broadcast_to([P, 4, 64]), op=ALU.mult))(par),
                    R=[b_PT[par], b_const], W=[b_PT[par]])
                S.op("act", (lambda par: lambda e: e.activation(out=PT[0:64, 2 + par, 0:256], in_=ps[0:64, 2 + par, 0:256],
                                                                func=AF.Exp, scale=0.125))(par),
                     R=[b_ps[2 + par]], W=[b_PT[2 + par]])
                S.op("dve", (lambda par: lambda e: e.tensor_tensor(
                    out=PT[0:64, 2 + par, 0:256].rearrange("p (h q) -> p h q", h=4),
                    in0=PT[0:64, 2 + par, 0:256].rearrange("p (h q) -> p h q", h=4),
                    in1=masks[0:64, 3, 64:128].unsqueeze(1).broadcast_to([64, 4, 64]), op=ALU.mult))(par),
                    R=[b_PT[2 + par], b_const], W=[b_PT[2 + par]])
                yield
            fns = []
            for h in range(8):
                fns.append((lambda h: lambda e: e.matmul(
                    ps[0:65, 4, h * 64:(h + 1) * 64], lhsT=vaug[cur][0:64, h // 4, :],
                    rhs=PT[0:64, 2 + h % 2, (h // 2) * 64:(h // 2 + 1) * 64],
                    start=(h == 0), stop=False, skip_group_check=True))(h))
            for b in range(16):
                for h in range(8):
                    last = (b == 15 and h == 7)
                    fns.append((lambda b, h, last: lambda e: e.matmul(
                        ps[0:65, 4, h * 64 + b:h * 64 + 64:16],
                        lhsT=vaugc[:, b, h // 4, :],
                        rhs=PT[:, h % 2, (h // 2) * 64 + b:(h // 2) * 64 + 64:16],
                        start=False, stop=last, skip_group_check=True))(b, h, last))
            S.group("pe", fns, R=[b_vaug[cur], b_vaugc] + b_PT, W=[b_ps[4]])
            yield
            S.op("act", lambda e: e.activation(out=oTs[0:65, :], in_=ps[0:65, 4, :], func=AF.Copy), R=[b_ps[4]], W=[b_oTs])
            yield
            for g in range(2):
                S.group("pe", [(lambda hh, g: lambda e: e.transpose(
                    ps[0:64, 5 + g, hh * 65:(hh + 1) * 65], oTs[0:65, (4 * g + hh) * 64:(4 * g + hh + 1) * 64],
                    identf[0:65, 0:65]))(hh, g) for hh in range(4)], R=[b_oTs, b_c2], W=[b_ps[5 + g]])
                yield
        rms_stats(UG[:, 512:1024], [b_UG], 512, 6, eps_mul=4.0)
        yield
        if gt >= NTILES - 2:
            S.op("dve", lambda e: e.scalar_tensor_tensor(out=gvn[:, :], in0=UG[:, 512:1024], scalar=stat[:, 6:7],
                                                         in1=g512[:, 0, :], op0=ALU.mult, op1=ALU.mult),
                 R=[b_UG, b_stat[6], b_const], W=[b_gvn])
            yield
            S.op("act", lambda e: e.activation(out=gvnb[:, :], in_=gvn[:, :], func=AF.Copy), R=[b_gvn], W=[b_gvnb])
            yield
        else:
            S.op("dve", lambda e: e.scalar_tensor_tensor(out=gvnb[:, :], in0=UG[:, 512:1024], scalar=stat[:, 6:7],
                                                         in1=g512[:, 0, :], op0=ALU.mult, op1=ALU.mult),
                 R=[b_UG, b_stat[6], b_const], W=[b_gvnb])
            yield
            yield
        if gt == NTILES - 2:
            out_toks.append(S.dma("sp", lambda e: e.dma_start(out=gvp_o[:, :], in_=gvn[:, :]), "o_g", R=[b_gvn]))
        if sample:
            out_toks.append(S.dma("sp", lambda e: e.dma_start(out=gvs_o[:, :], in_=gvn[0:64, :]), "o_g", R=[b_gvn]))
            S.group("pe", [(lambda h: lambda e: e.matmul(
                ps[0:64, 7, h * 64:(h + 1) * 64], lhsT=wsamp[:, h, :], rhs=gvnb[0:64, h * 64:(h + 1) * 64],
                start=True, stop=True, skip_group_check=True))(h) for h in range(8)], R=[b_gvnb, b_c2], W=[b_ps[7]])
            bcol = 8
        else:
            S.group("pe", [(lambda h: lambda e: e.matmul(
                ps[:, 7, h * 64:(h + 1) * 64], lhsT=wsT[:, h, :], rhs=gvnb[:, h * 64:(h + 1) * 64],
                start=True, stop=True, skip_group_check=True))(h) for h in range(8)], R=[b_gvnb, b_c2], W=[b_ps[7]])
            bcol = 0
        S.op("dve", lambda e: e.tensor_tensor(
            out=yg[:, :].rearrange("p (h d) -> p h d", d=64), in0=ps[:, 7, :].rearrange("p (h d) -> p h d", d=64),
            in1=bsT[:, bcol:bcol + 8].unsqueeze(2).broadcast_to([P, 8, 64]), op=ALU.add),
            R=[b_ps[7], b_const], W=[b_yg])
        yield
        for g in range(2):
            pv = ps[:, 5 + g, 0:260].rearrange("p (h e) -> p h e", e=65)
            S.op("dve", (lambda g, pv: lambda e: e.tensor_tensor(
                out=stat[:, 8 + 4 * g:12 + 4 * g].unsqueeze(2), in0=pv[:, :, 64:65],
                in1=esink[:, 4 * g:4 * g + 4].unsqueeze(2), op=ALU.add))(g, pv),
                R=[b_ps[5 + g], b_c2], W=[b_stat[8 + g]])
            S.op("dve", (lambda g: lambda e: e.reciprocal(out=stat[:, 8 + 4 * g:12 + 4 * g],
                                                          in_=stat[:, 8 + 4 * g:12 + 4 * g]))(g),
                 R=[b_stat[8 + g]], W=[b_stat[8 + g]])
            S.op("dve", (lambda g, pv: lambda e: e.tensor_tensor(
                out=ya[:, g * 256:(g + 1) * 256].rearrange("p (h d) -> p h d", d=64), in0=pv[:, :, 0:64],
                in1=stat[:, 8 + 4 * g:12 + 4 * g].unsqueeze(2).broadcast_to([P, 4, 64]), op=ALU.mult))(g, pv),
                R=[b_ps[5 + g], b_stat[8 + g]], W=[b_ya])
            yield
        rms_stats(ya[:, :], [b_ya], 512, 5)
        yield
        S.op("dve", lambda e: e.scalar_tensor_tensor(out=yan[:, :], in0=ya[:, :], scalar=stat[:, 5:6], in1=g512[:, 1, :],
                                                     op0=ALU.mult, op1=ALU.mult), R=[b_ya, b_stat[5], b_const], W=[b_yan])
        yield
        S.op("pool", lambda e: e.tensor_tensor(out=yg[:, :], in0=yg[:, :], in1=UG[:, 0:512], op=ALU.mult),
             R=[b_yg, b_UG], W=[b_yg])
        yield
        rms_stats(yg[:, :], [b_yg], 512, 7, eps_mul=4.0)
        yield
        S.op("dve", lambda e: e.scalar_tensor_tensor(out=ygn[:, :], in0=yg[:, :], scalar=stat[:, 7:8], in1=g512[:, 2, :],
                                                     op0=ALU.mult, op1=ALU.mult), R=[b_yg, b_stat[7], b_const], W=[b_ygn])
        yield

    def mix_tail(t, gt):
        S.group("pe", [(lambda j: lambda e: e.transpose(
            bank_bf(7)[:, j * 128:(j + 1) * 128], (yan if j < 4 else ygn)[:, (j % 4) * 128:(j % 4 + 1) * 128],
            identb[:, :]))(j) for j in range(8)], R=[b_yan, b_ygn, b_c2], W=[b_ps[7]])
        S.op("act", lambda e: e.activation(out=cT[:, :, :], in_=bank_bf(7).rearrange("p (j q) -> p j q", j=8),
                                           func=AF.Copy), R=[b_ps[7]], W=[b_cT])
        yield
        for nh in range(2):
            S.group("pe", [(lambda kc, nh: lambda e: e.matmul(
                ps[:, 7, :], lhsT=cT[:, kc, :],
                rhs=hid[:, 14336 + kc * DM + nh * 512: 14336 + kc * DM + (nh + 1) * 512],
                start=(kc == 0), stop=(kc == KC - 1)))(kc, nh) for kc in range(KC)],
                R=[b_cT] + b_wout, W=[b_ps[7]])
            S.op("dve", (lambda nh: lambda e: e.tensor_tensor(
                out=xres[:, t, nh * 512:(nh + 1) * 512], in0=ps[:, 7, :], in1=xres[:, t, nh * 512:(nh + 1) * 512],
                op=ALU.add))(nh), R=[b_ps[7], b_x[t]], W=[b_x[t]])
            yield

    PATTERN = os.environ.get("MK_PATTERN") or (
        "TF" "BBBB" "T" "BB" "BBBB" "T" "BB" "FF" "BB" "F" "B" "F" "B" "F" "B" "FF" "B" "F" "B" "F" "B" "FF"
        "BB" "F" "B" "F" "BB" "FFFFF")

    def interleave(fr, bk, tl):
        gens = {"F": iter(fr) if fr is not None else None,
                "B": iter(bk) if bk is not None else None,
                "T": iter(tl) if tl is not None else None}

        def step(which):
            g = gens[which]
            if g is None:
                return
            try:
                next(g)
            except StopIteration:
                gens[which] = None

        for ch in PATTERN:
            step(ch)
        while any(g is not None for g in gens.values()):
            step("T")
            step("B")
            step("F")

    emit_const_loads()
    for p in range(NPASS):
        for t in range(NT):
            gt = p * NT + t
            S.dma("pool" if p == 0 else "sp",
                  (lambda t, gt: lambda e: e.dma_start(out=xres[:, t, :], in_=xin[gt, :, :]))(t, gt),
                  f"x{t}", W=[b_x[t]])
        if p == 0:
            segs1 = [(0, 384), (384, 384), (768, 384)]
            segs2 = [(128, 384), (512, 384), (896, 256)]
        else:
            segs1 = segs2 = [(0, 384), (384, 384), (768, int(os.environ.get("MK_LASTW", "320")))]
        if p == 0:
            gu_dma(0, 0)
            gu_dma(0, 1)
            norm_stats_all(use_dve=False)
            emit_setup()
            ffn(0, True, segs1, stats_done=True, mid_hook=emit_cache_prep)
        else:
            ffn(0, False, segs1)
        load_weights_mix("winA")
        norm_stats_all()
        load_weights_mix("winB")
        norm_apply_batch(list(range(NT)), 1)
        load_weights_mix("wout")
        gu_dma(1, 0)
        gu_dma(1, 1)
        pend_b = pend_t = None
        for t in range(NT):
            gt = p * NT + t
            interleave(mix_front(t, gt), pend_b, pend_t)
            if t - 2 >= 0 and gt - 2 != 0:
                norm_single(t - 2, 2, 4)
            pend_t = mix_tail(t - 1, gt - 1) if (t >= 1 and gt - 1 != 0) else None
            pend_b = mix_back(t, gt) if gt != 0 else None
        interleave(None, pend_b, pend_t)
        norm_single(NT - 2, 2, 4)
        interleave(None, None, mix_tail(NT - 1, p * NT + NT - 1))
        norm_single(NT - 1, 2, 4)
        ffn(1, True, segs2, norms_done=True)
        norm_stats_all([t for t in range(NT) if p * NT + t != 0], use_dve=False)
        for t in range(NT):
            gt = p * NT + t
            if gt == 0:
                continue
            S.op("dve", (lambda t: lambda e: e.scalar_tensor_tensor(
                out=xres[:, t, :], in0=xres[:, t, :], scalar=stat[:, 16 + t:17 + t], in1=gfin[:, :],
                op0=ALU.mult, op1=ALU.mult))(t), R=[b_x[t], b_stat[16 + t], b_const], W=[b_x[t]])
            out_toks.append(S.dma("sp", (lambda t, gt: lambda e: e.dma_start(
                out=yout[gt - 1, :, :], in_=xres[:, t, :]))(t, gt), f"xo{t}", R=[b_x[t]]))
    out_toks.append(S.dma("sp", lambda e: e.dma_start(out=kws_o[:, 0:124, :], in_=ck_d[:, 4:128, :]), "o_c"))
    out_toks.append(S.dma("sp", lambda e: e.dma_start(out=vws_o[:, 0:124, :], in_=cv_d[:, 4:128, :]), "o_c"))
    S.wait_only("sp", out_toks)

    keys = sorted(S.cnt.keys())
    sems = {k: es.enter_context(nc.semaphore(f"s_{k}")) for k in keys}
    with nc.Block() as block:
        @block.tensor
        def _(e):
            S.emit("pe", e, sems)

        @block.scalar
        def _(e):
            S.emit("act", e, sems)

        @block.vector
        def _(e):
            S.emit("dve", e, sems)

        @block.gpsimd
        def _(e):
            S.emit("pool", e, sems)

        @block.sync
        def _(e):
            S.emit("sp", e, sems)
    es.close()
    return nc


def _rope_tables():
    inv = (10000.0 ** (-np.arange(0, 64, 2, dtype=np.float32) / np.float32(64))).astype(np.float32)
    return inv


def _host_inputs(inp):
    f32 = np.float32
    bf = ml_dtypes.bfloat16
    inv = _rope_tables()
    xp = np.asarray(inp["x_prompt"], f32)
    xsm = np.asarray(inp["x_sample"], f32)
    ck = np.asarray(inp["cache_k_win"], f32)[0].reshape(128, 128, 128)
    cv = np.asarray(inp["cache_v_win"], f32)[0].reshape(128, 128, 128)

    def rep128(v):
        return np.ascontiguousarray(np.broadcast_to(np.asarray(v, f32)[None, :], (128, v.shape[-1])))

    shared = {}
    for f, (g, u, d) in enumerate((("ffn1_gate", "ffn1_up", "ffn1_down"), ("ffn2_gate", "ffn2_up", "ffn2_down"))):
        wg = np.asarray(inp[g], f32)[0].reshape(8, 128, 22, 128)
        wu = np.asarray(inp[u], f32)[0].reshape(8, 128, 22, 128)
        gu = np.stack([wg, wu], 0)
        shared[f"wgu{f}"] = np.ascontiguousarray(gu.transpose(3, 2, 0, 1, 4).reshape(22, 128, 2048))
        wd = np.asarray(inp[d], f32)[0].reshape(22, 128, 8, 128)
        shared[f"wd{f}"] = np.ascontiguousarray(wd.transpose(2, 1, 0, 3).reshape(8, 128, 2816))
    shared["win"] = np.ascontiguousarray(np.asarray(inp["w_in"], f32)[0].reshape(8, 128, 1792))
    shared["wout"] = np.ascontiguousarray(np.asarray(inp["w_out"], f32)[0].reshape(8, 128, 1024))
    shared["gfin"] = rep128(inp["norm_final"])
    shared["gfm"] = np.ascontiguousarray(np.concatenate(
        [np.asarray(inp[k], f32)[0].reshape(8, 128).T for k in ("norm_ffn1", "norm_mix", "norm_ffn2")], axis=1))
    shared["g512"] = np.stack([rep128(inp["gmlp_v_norm"][0]), rep128(inp["norm_attn_out"][0]),
                               rep128(inp["norm_gmlp_out"][0])], 0)
    shared["sinks"] = rep128(inp["attn_sinks"][0])
    ws = np.asarray(inp["gmlp_w_s"], f32)[0]
    shared["wsT"] = np.ascontiguousarray(ws.transpose(2, 0, 1).reshape(128, 1024))
    s_i = np.arange(128)
    shared["trilT"] = (s_i[:, None] <= s_i[None, :]).astype(f32)
    bs = np.asarray(inp["gmlp_b_s"], f32)[0]
    bsT = np.zeros((128, 16), f32)
    bsT[:, 0:8] = bs.T
    bsT[0:64, 8:16] = np.repeat(bs[:, 0:4].T, 16, axis=0)
    shared["bsT"] = bsT
    shared["ws4"] = np.ascontiguousarray(ws[:, 0:4, 0:4].transpose(1, 0, 2).reshape(4, 32))
    t4 = np.arange(4)
    shared["tril4"] = np.ascontiguousarray(
        np.broadcast_to((t4[None, None, :] <= t4[:, None, None]), (4, 8, 4)).astype(f32).reshape(4, 32))
    rep = np.zeros((4, 64), f32)
    for t in range(4):
        rep[t, t * 16:(t + 1) * 16] = 1
    shared["rep"] = rep.astype(bf)
    tb = np.arange(64)
    shared["dmask"] = ((tb[:, None] % 16) == (tb[None, :] % 16)).astype(f32).astype(bf)

    base = np.zeros((128, 5, 128), f32)
    base[:, 0, :] = (s_i[:, None] > s_i[None, :])
    base[:, 1, :] = (s_i[:, None] <= s_i[None, :])
    tq = tb // 16
    base[:, 3, 0:64] = (s_i[:, None] >= (tq[None, :] + 1))
    base[0:64, 3, 64:128] = ((tb[:, None] % 16) == (tb[None, :] % 16)) & ((tb[:, None] // 16) <= (tb[None, :] // 16))
    base[:, 4, :] = np.eye(128, dtype=f32)

    in_maps = []
    for c in range(8):
        b, half = c // 2, c % 2
        xin = np.zeros((NTILES, 128, 1024), f32)
        pos = np.zeros((NTILES, 128), np.int64)
        if half == 1:
            xin[0] = xp[b, 2048 - 128:2048]
            pos[0] = 2048 - 128 + np.arange(128)
        xin[1:17] = xp[b, half * 2048:(half + 1) * 2048].reshape(16, 128, 1024)
        pos[1:17] = (half * 2048 + np.arange(2048)).reshape(16, 128)
        xin[17, 0:64] = xsm[16 * c:16 * c + 16].transpose(1, 0, 2).reshape(64, 1024)
        pos[17, 0:64] = PAST_LEN + np.arange(64) // 16
        ang = (pos.astype(f32)[:, :, None] * inv[None, None, :]).astype(f32)
        co = np.cos(ang.astype(np.float64)).astype(f32)
        si = np.sin(ang.astype(np.float64)).astype(f32)
        rope = np.concatenate([co, co, -si, si], axis=-1).astype(f32)
        m = base.copy()
        if half == 1:
            m[:, 2, :] = base[:, 0, :]
        d = dict(shared)
        d["xin"] = xin
        d["rope"] = np.ascontiguousarray(rope)
        d["masks"] = np.ascontiguousarray(m.reshape(128, 640)).astype(bf)
        d["ck"] = np.ascontiguousarray(ck[16 * c:16 * c + 16])
        d["cv"] = np.ascontiguousarray(cv[16 * c:16 * c + 16])
        in_maps.append(d)
    return in_maps


_NC_CACHE = {}


def kernel(**inputs):
    in_maps = _host_inputs(inputs)
    if "nc" not in _NC_CACHE:
        _NC_CACHE["nc"] = build_program()
    nc = _NC_CACHE["nc"]
    res = run_bass_kernel_spmd(nc, in_maps, core_ids=list(range(8)))
    R = res.results
    f32 = np.float32
    y_prompt = np.zeros((4, 4096, 1024), f32)
    y_sample = np.zeros((128, 4, 1024), f32)
    kwp = np.zeros((1, 4, 128, 2, 64), f32)
    vwp = np.zeros((1, 4, 128, 2, 64), f32)
    kws = np.zeros((1, 128, 128, 2, 64), f32)
    vws = np.zeros((1, 128, 128, 2, 64), f32)
    gvp = np.zeros((1, 4, 128, 8, 64), f32)
    gvs = np.zeros((1, 128, 4, 8, 64), f32)
    for c in range(8):
        b, half = c // 2, c % 2
        r = R[c]
        yo = np.asarray(r["yout"], f32)
        y_prompt[b, half * 2048:(half + 1) * 2048] = yo[0:16].reshape(2048, 1024)
        y_sample[16 * c:16 * c + 16] = yo[16, 0:64].reshape(4, 16, 1024).transpose(1, 0, 2)
        if half == 1:
            kwp[0, b] = np.asarray(r["kwp"], f32).reshape(128, 2, 64)
            vwp[0, b] = np.asarray(r["vwp"], f32).reshape(128, 2, 64)
            gvp[0, b] = np.asarray(r["gvp"], f32).reshape(128, 8, 64)
        kws[0, 16 * c:16 * c + 16] = np.asarray(r["kws"], f32).reshape(16, 128, 2, 64)
        vws[0, 16 * c:16 * c + 16] = np.asarray(r["vws"], f32).reshape(16, 128, 2, 64)
        gvs[0, 16 * c:16 * c + 16] = np.asarray(r["gvs"], f32).reshape(4, 16, 8, 64).transpose(1, 0, 2, 3)
    return (y_prompt, y_sample, kwp, vwp, kws, vws, gvp, gvs)
```

```python
import contextlib
import os
import numpy as np
import ml_dtypes
import concourse.bass as bass
import concourse.mybir as mybir
from concourse.bass_utils import run_bass_kernel_spmd

F32 = mybir.dt.float32
BF16 = mybir.dt.bfloat16
AF = mybir.ActivationFunctionType
ALU = mybir.AluOpType

P = 128
NT = 9
NPASS = 2
TOK = NT * P
TS = 384
NTS = TOK // TS
DM = 1024
KC = 8
DFF = 2816
NFC = 22
DIN = 1792
PAST_LEN = 16384
EPS = 1e-6
GELU_C1 = 1.5957691216057308
GELU_C0 = 0.7978845608028654
GELU_C2 = 0.044715
NTILES = NT * NPASS


class Buf:
    __slots__ = ("name", "w", "r", "excl", "strict")

    def __init__(self, name, excl=False, strict=False):
        self.name = name
        self.w = None
        self.r = []
        self.excl = excl
        self.strict = strict


class Sched:
    ENG = ("pe", "act", "dve", "pool", "sp")

    def __init__(self):
        self.q = {e: [] for e in self.ENG}
        self.cnt = {}
        self.seen = {e: {} for e in self.ENG}

    def _need(self, eng, waits, tok, raw):
        if tok is None:
            return
        k, v = tok
        if k == eng and not raw:
            return
        if self.seen[eng].get(k, 0) >= v:
            return
        if waits.get(k, 0) < v:
            waits[k] = v

    def _deps(self, eng, R, W, extra):
        waits = {}
        for b in R:
            self._need(eng, waits, b.w, True)
            if b.excl:
                for t in b.r:
                    self._need(eng, waits, t, False)
        for b in W:
            self._need(eng, waits, b.w, b.strict)
            for t in b.r:
                self._need(eng, waits, t, False)
        for t in extra:
            self._need(eng, waits, t, True)
        for k, v in waits.items():
            self.seen[eng][k] = v
        return list(waits.items())

    def _commit(self, tok, R, W):
        for b in R:
            b.r.append(tok)
        for b in W:
            b.w = tok
            b.r = []

    def op(self, eng, fn, R=(), W=(), extra=()):
        return self.group(eng, [fn], R, W, extra)

    def group(self, eng, fns, R=(), W=(), extra=()):
        waits = self._deps(eng, R, W, extra)
        n = self.cnt.get(eng, 0) + 1
        self.cnt[eng] = n
        tok = (eng, n)
        for i, fn in enumerate(fns):
            self.q[eng].append((fn, waits if i == 0 else [], (eng, 1) if i == len(fns) - 1 else None))
        self._commit(tok, R, W)
        return tok

    def dma(self, eng, fn, sem, R=(), W=(), extra=()):
        waits = self._deps(eng, R, W, extra)
        n = self.cnt.get(sem, 0) + 16
        self.cnt[sem] = n
        tok = (sem, n)
        self.q[eng].append((fn, waits, (sem, 16)))
        self._commit(tok, R, W)
        return tok

    def wait_only(self, eng, toks):
        waits = {}
        for t in toks:
            self._need(eng, waits, t, True)
        for k, v in waits.items():
            self.seen[eng][k] = v
        self.q[eng].append((None, list(waits.items()), None))

    def emit(self, eng, e, sems):
        for fn, waits, inc in self.q[eng]:
            for k, v in waits:
                e.wait_ge(sems[k], v)
            if fn is None:
                continue
            ins = fn(e)
            if inc is not None:
                ins.then_inc(sems[inc[0]], inc[1])


def build_program():
    nc = bass.Bass("TRN2", target_bir_lowering=False)
    S = Sched()
    es = contextlib.ExitStack()

    def din(name, shape, dt=F32):
        return nc.dram_tensor(name, list(shape), dt, kind="ExternalInput").ap()

    def dout(name, shape, dt=F32):
        return nc.dram_tensor(name, list(shape), dt, kind="ExternalOutput").ap()

    def sb(name, shape, dt):
        return es.enter_context(nc.sbuf_tensor(name, list(shape), dt))

    xin = din("xin", [NTILES, P, DM])
    rope = din("rope", [NTILES, P, 128])
    masks_d = din("masks", [P, 5 * 128], BF16)
    ck_d = din("ck", [16, P, 128])
    cv_d = din("cv", [16, P, 128])
    wgu_d = [din(f"wgu{f}", [NFC, P, 2048]) for f in range(2)]
    wd_d = [din(f"wd{f}", [KC, P, DFF]) for f in range(2)]
    win_d = din("win", [KC, P, DIN])
    wout_d = din("wout", [KC, P, DM])
    gfin_d = din("gfin", [P, DM])
    gfm_d = din("gfm", [P, 24])
    g512_d = din("g512", [3, P, 512])
    sinks_d = din("sinks", [P, 8])
    wsT_d = din("wsT", [P, 8 * 128])
    trilT_d = din("trilT", [P, 128])
    bsT_d = din("bsT", [P, 16])
    ws4_d = din("ws4", [4, 32])
    tril4_d = din("tril4", [4, 32])
    rep_d = din("rep", [4, 64], BF16)
    dmask_d = din("dmask", [64, 64], BF16)

    yout = dout("yout", [NTILES - 1, P, DM])
    kwp_o = dout("kwp", [P, 128])
    vwp_o = dout("vwp", [P, 128])
    gvp_o = dout("gvp", [P, 512])
    kws_o = dout("kws", [16, P, 128])
    vws_o = dout("vws", [16, P, 128])
    gvs_o = dout("gvs", [64, 512])

    xres = sb("xres", [P, NT, DM], F32)
    hT = sb("hT", [P, KC, TOK], BF16)
    hid = sb("hid", [P, NFC * TOK], BF16)
    gus = [sb(f"gus{i}", [P, 2048], BF16) for i in range(2)]
    wds = [sb(f"wds{i}", [P, DFF], BF16) for i in range(2)]
    xs = [sb(f"xs{i}", [P, DM], BF16) for i in range(2)]
    junk = sb("junk", [P, DM], BF16)
    mhalf = sb("mhalf", [P, 1], F32)
    silu_t = [sb(f"silu{i}", [P, TS], F32) for i in range(2)]
    dn_t = [sb(f"dn{i}", [P, TS], F32) for i in range(2)]
    stat = sb("stat", [P, 32], F32)
    gfin = sb("gfins", [P, DM], F32)
    gfm = sb("gfms", [P, 3, 8], F32)
    g512 = sb("g512s", [P, 3, 512], F32)
    gl1 = sb("gl1", [P, 1024], F32)
    wsTr = gl1
    esink = sb("esink", [P, 8], F32)
    wsT = sb("wsTb", [P, 8, 128], BF16)
    trilT = sb("trilTs", [P, 128], F32)
    bsT = sb("bsTs", [P, 16], F32)
    ws4r = sb("ws4r", [4, 32], F32)
    tril4 = sb("tril4s", [4, 32], F32)
    ws4b = sb("ws4b", [4, 8, 4], BF16)
    rep = sb("reps", [4, 64], BF16)
    dmask = sb("dmasks", [64, 64], BF16)
    bsamp = sb("bsamp", [4, 8 * 64], BF16)
    wsamp = sb("wsamp", [64, 8, 64], BF16)
    masks = sb("maskss", [P, 5, 128], BF16)
    identb = sb("identb", [P, 128], BF16)
    identf = sb("identf", [P, 128], F32)
    ropet = [sb(f"ropet{i}", [P, 128], F32) for i in range(2)]
    qkr = sb("qkr", [P, 768], BF16)
    kro = sb("kro", [P, 128], F32)
    vro = sb("vro", [P, 128], F32)
    vaug = [sb(f"vaug{i}", [P, 2, 65], BF16) for i in range(3)]
    kT = [sb(f"kT{i}", [P, 2, 128], BF16) for i in range(3)]
    qT = [sb(f"qT{i}", [P, 4, 128], BF16) for i in range(2)]
    PT = sb("PT", [P, 4, 512], BF16)
    ya = sb("ya", [P, 512], F32)
    yan = sb("yan", [P, 512], BF16)
    ugv = [sb(f"ugv{i}", [P, 1024], F32) for i in range(2)]
    ropeR = sb("ropeR", [P, 640], F32)
    gvn = sb("gvn", [P, 512], F32)
    gvnb = sb("gvnb", [P, 512], BF16)
    yg = sb("yg", [P, 512], F32)
    ygn = sb("ygn", [P, 512], BF16)
    cT = sb("cT", [P, 8, 128], BF16)
    kTc = sb("kTc", [P, 16, 2, 128], BF16)
    vaugc = sb("vaugc", [P, 16, 2, 65], BF16)
    oTs = ya
    ps = es.enter_context(nc.psum_tensor("ps", [P, 8, 512], F32))

    def bank(i):
        return ps[:, i, :]

    def bank_bf(i):
        return ps[:, i, :].bitcast(BF16)

    B = lambda n: Buf(n)
    b_x = [B(f"x{i}") for i in range(NT)]
    b_hT = [B(f"hT{i}") for i in range(NT)]
    b_hid = [B(f"hid{i}") for i in range(NFC)]
    b_win = b_hid[0:13]
    b_wout = b_hid[12:20]
    b_gus = [B("gus0"), B("gus1")]
    b_wds = [B("wds0"), B("wds1")]
    b_xs = [B("xs0"), B("xs1")]
    b_junk = Buf("junk", strict=True)
    b_j2 = Buf("j2", strict=True)
    b_silu = [B("silu0"), B("silu1")]
    b_dn = [B("dn0"), B("dn1")]
    b_stat = [B(f"stat{i}") for i in range(32)]
    b_const = B("const")
    b_ropet = [B("ropet0"), B("ropet1")]
    b_qkr, b_kro, b_vro = B("qkr"), B("kro"), B("vro")
    b_vaug = [B("vaug0"), B("vaug1"), B("vaug2")]
    b_kT = [B("kT0"), B("kT1"), B("kT2")]
    b_ya, b_yan, b_gl1 = B("ya"), B("yan"), B("gl1")
    b_qT = [B("qT0"), B("qT1")]
    b_ugv = [B("ugv0"), B("ugv1")]
    b_PT = [B(f"PT{i}") for i in range(4)]
    b_gvn, b_gvnb, b_yg, b_ygn, b_cT = B("gvn"), B("gvnb"), B("yg"), B("ygn"), B("cT")
    b_kTc, b_vaugc = B("kTc"), B("vaugc")
    b_oTs = b_ya
    b_ropeR = B("ropeR")
    b_ps = [Buf(f"ps{i}", excl=True) for i in range(8)]
    out_toks = []

    b_c2 = B("const2")
    b_mh = B("mhalf")
    S.op("dve", lambda e: e.memset(mhalf[:, :], -0.5), W=[b_mh])
    cache_views = {}

    def emit_const_loads():
        def load(eng, dst, src, sem, W, R=()):
            if sem == "c0":
                tok = S.dma(eng, lambda e: e.dma_start(out=dst, in_=src), sem)
                b_const.w = tok
                return tok
            return S.dma(eng, lambda e: e.dma_start(out=dst, in_=src), sem, R=R, W=W)

        load("sp", gfin[:, :], gfin_d[:, :], "c0", [b_const])
        load("sp", gfm[:, :, :], gfm_d.rearrange("p (g k) -> p g k", g=3), "c0", [b_const])
        load("sp", g512[:, :, :], g512_d.rearrange("g p d -> p g d"), "c0", [b_const])
        load("sp", esink[:, :], sinks_d[:, :], "c0", [b_const])
        load("sp", wsTr[:, :], wsT_d[:, :], "c0", [b_const])
        load("sp", trilT[:, :], trilT_d[:, :], "c0", [b_const])
        load("sp", bsT[:, :], bsT_d[:, :], "c0", [b_const])
        load("sp", ws4r[:, :], ws4_d[:, :], "c0", [b_const])
        load("sp", tril4[:, :], tril4_d[:, :], "c0", [b_const])
        load("sp", rep[:, :], rep_d[:, :], "c0", [b_const])
        load("sp", dmask[:, :], dmask_d[:, :], "c0", [b_const])
        load("sp", masks[:, :, :], masks_d.rearrange("p (m q) -> p m q", m=5), "c0", [b_const])

    def emit_setup():
        S.op("dve", lambda e: e.tensor_copy(out=identb[:, :], in_=masks[:, 4, :]), R=[b_const], W=[b_c2])
        S.op("dve", lambda e: e.tensor_copy(out=identf[:, :], in_=masks[:, 4, :]), R=[b_const], W=[b_c2])
        S.op("act", lambda e: e.activation(out=esink[:, :], in_=esink[:, :], func=AF.Exp), R=[b_const], W=[b_c2])
        S.op("dve", lambda e: e.tensor_tensor(
            out=wsT[:, :, :], in0=wsTr[:, :].rearrange("p (h t) -> p h t", h=8),
            in1=trilT[:, :].unsqueeze(1).broadcast_to([P, 8, 128]), op=ALU.mult), R=[b_const, b_gl1], W=[b_c2])
        S.op("dve", lambda e: e.tensor_tensor(
            out=ws4b[:, :, :], in0=ws4r[:, :].rearrange("p (h s) -> p h s", h=8),
            in1=tril4[:, :].rearrange("p (h s) -> p h s", h=8), op=ALU.mult), R=[b_const], W=[b_c2])
        for i in range(3):
            S.op("dve", (lambda i: lambda e: e.memset(vaug[i][:, :, 64:65], 1.0))(i), W=[b_vaug[i]])
        S.op("dve", lambda e: e.memset(vaugc[:, :, :, 64:65], 1.0), W=[b_vaugc])

    def emit_setup_b():
        S.group("pe", [(lambda h: lambda e: e.matmul(
            ps[0:4, 6, h * 64:(h + 1) * 64], lhsT=ws4b[:, h, :], rhs=rep[:, :], start=True, stop=True))(h)
            for h in range(8)], R=[b_c2, b_const], W=[b_ps[6]])
        S.op("act", lambda e: e.activation(out=bsamp[:, :], in_=ps[0:4, 6, :], func=AF.Copy), R=[b_ps[6]], W=[b_c2])
        S.op("pe", lambda e: e.matmul(ps[0:64, 7, :], lhsT=rep[:, :], rhs=bsamp[:, :], start=True, stop=True),
             R=[b_c2, b_const], W=[b_ps[7]])
        S.op("dve", lambda e: e.tensor_tensor(
            out=wsamp[:, :, :], in0=ps[0:64, 7, :].rearrange("p (h t) -> p h t", h=8),
            in1=dmask[:, :].unsqueeze(1).broadcast_to([64, 8, 64]), op=ALU.mult), R=[b_ps[7], b_const], W=[b_c2])


    def emit_cache_prep(c):
        kst, vst = gl1, ugv[0]
        kb = ugv[1][:, :].bitcast(BF16)

        def dmas(hf):
            S.dma("sp", lambda e: e.dma_start(
                out=kst[:, :].rearrange("p (b d) -> p b d", b=8),
                in_=ck_d[hf * 8:(hf + 1) * 8, :, :].rearrange("b s d -> s b d")), "c1", W=[b_gl1])
            S.dma("sp", lambda e: e.dma_start(
                out=vst[:, :].rearrange("p (b d) -> p b d", b=8),
                in_=cv_d[hf * 8:(hf + 1) * 8, :, :].rearrange("b s d -> s b d")), "c2", W=[b_ugv[0]])

        def compute(hf):
            S.op("dve", lambda e: e.tensor_copy(
                out=kb.rearrange("p (bg u d) -> p bg u d", u=2, d=64),
                in_=kst[:, :].rearrange("p (bg d) -> p bg d", d=64).unsqueeze(2).broadcast_to([P, 16, 2, 64])),
                R=[b_gl1], W=[b_ugv[1]])
            S.op("act", lambda e: e.activation(
                out=vaugc[:, hf * 8:(hf + 1) * 8, :, 0:64],
                in_=vst[:, :].rearrange("p (b g d) -> p b g d", b=8, g=2), func=AF.Copy),
                R=[b_ugv[0]], W=[b_vaugc])
            for q2 in range(2):
                bk = 6 + q2
                S.group("pe", [(lambda j, bk: lambda e: e.transpose(
                    bank_bf(bk)[:, (j % 8) * 128:(j % 8 + 1) * 128], kb[:, j * 128:(j + 1) * 128],
                    identb[:, :]))(j, bk) for j in range(q2 * 8, q2 * 8 + 8)], R=[b_ugv[1], b_c2], W=[b_ps[bk]])
                S.op("act", (lambda q2, bk: lambda e: e.activation(
                    out=kTc[:, hf * 8 + q2 * 4:hf * 8 + (q2 + 1) * 4, :, :],
                    in_=bank_bf(bk).rearrange("p (b g s) -> p b g s", b=4, g=2), func=AF.Copy))(q2, bk),
                    R=[b_ps[bk]], W=[b_kTc])

        if c == 0:
            emit_setup_b()
        elif c == 1:
            dmas(0)
        elif c == 4:
            compute(0)
            dmas(1)
        elif c == 7:
            compute(1)

    def rms_stats(src_ap, src_bufs, width, slot, eps_mul=1.0, on_dve=False):
        if on_dve:
            j2 = ropeR[:, 0:512].bitcast(BF16)
            S.op("dve", lambda e: e.scalar_tensor_tensor(out=j2[:, 0:width], in0=src_ap, scalar=1.0, in1=src_ap,
                                                         op0=ALU.mult, op1=ALU.mult,
                                                         accum_out=stat[:, slot:slot + 1]),
                 R=src_bufs, W=[b_stat[slot], b_ropeR, b_j2])
        else:
            S.op("act", lambda e: e.activation(out=junk[:, 0:width], in_=src_ap, func=AF.Square,
                                               accum_out=stat[:, slot:slot + 1]),
                 R=src_bufs, W=[b_stat[slot], b_junk])
        S.op("pool", lambda e: e.tensor_scalar(out=stat[:, slot:slot + 1], in0=stat[:, slot:slot + 1],
                                               scalar1=1.0 / width, scalar2=EPS * eps_mul,
                                               op0=ALU.mult, op1=ALU.add),
             R=[b_stat[slot]], W=[b_stat[slot]])
        S.op("pool", lambda e: e.tensor_tensor(out=stat[:, slot:slot + 1], in0=stat[:, slot:slot + 1],
                                               in1=mhalf[:, 0:1], op=ALU.pow),
             R=[b_stat[slot], b_mh], W=[b_stat[slot]])

    def norm_stats_all(tiles=None, use_dve=True):
        for i, t in enumerate(range(NT) if tiles is None else tiles):
            rms_stats(xres[:, t, :], [b_x[t]], DM, 16 + t, on_dve=(use_dve and i % 3 == 2))

    def norm_apply_batch(tiles, gi):
        n = len(tiles)

        def ts(i):
            t = tiles[i]
            S.op("dve", lambda e: e.tensor_scalar(out=xs[i % 2][:, :], in0=xres[:, t, :],
                                                  scalar1=stat[:, 16 + t:17 + t], scalar2=None, op0=ALU.mult),
                 R=[b_x[t], b_stat[16 + t]], W=[b_xs[i % 2]])

        def tr(i):
            bk = 6 + i % 2
            S.group("pe", [(lambda kc: lambda e: e.transpose(
                bank_bf(bk)[:, kc * 128:(kc + 1) * 128], xs[i % 2][:, kc * 128:(kc + 1) * 128], identb[:, :]))(kc)
                for kc in range(KC)], R=[b_xs[i % 2], b_c2], W=[b_ps[bk]])

        def mu(i):
            t = tiles[i]
            bk = 6 + i % 2
            S.op("dve", lambda e: e.tensor_tensor(
                out=hT[:, :, t * P:(t + 1) * P], in0=bank_bf(bk).rearrange("p (k q) -> p k q", k=KC),
                in1=gfm[:, gi, :].unsqueeze(2).broadcast_to([P, KC, 128]), op=ALU.mult),
                R=[b_ps[bk], b_const], W=[b_hT[t]])

        for i in range(n + 2):
            if i < n:
                ts(i)
            if 1 <= i <= n:
                tr(i - 1)
            if 2 <= i <= n + 1:
                mu(i - 2)

    def norm_single(t, gi, bk):
        rms_stats(xres[:, t, :], [b_x[t]], DM, 16 + t)
        S.op("dve", lambda e: e.tensor_scalar(out=xs[0][:, :], in0=xres[:, t, :],
                                              scalar1=stat[:, 16 + t:17 + t], scalar2=None, op0=ALU.mult),
             R=[b_x[t], b_stat[16 + t]], W=[b_xs[0]])
        S.group("pe", [(lambda kc: lambda e: e.transpose(
            bank_bf(bk)[:, kc * 128:(kc + 1) * 128], xs[0][:, kc * 128:(kc + 1) * 128], identb[:, :]))(kc)
            for kc in range(KC)], R=[b_xs[0], b_c2], W=[b_ps[bk]])
        S.op("dve", lambda e: e.tensor_tensor(
            out=hT[:, :, t * P:(t + 1) * P], in0=bank_bf(bk).rearrange("p (k q) -> p k q", k=KC),
            in1=gfm[:, gi, :].unsqueeze(2).broadcast_to([P, KC, 128]), op=ALU.mult),
            R=[b_ps[bk], b_const], W=[b_hT[t]])

    def gu_dma(f, c):
        sl = c % 2
        S.dma("pool", lambda e: e.dma_start(out=gus[sl][:, :], in_=wgu_d[f][c, :, :]), f"gu{sl}", W=[b_gus[sl]])

    def wd_dma(f, oc):
        sl = oc % 2
        S.dma("pool", lambda e: e.dma_start(out=wds[sl][:, :], in_=wd_d[f][oc, :, :]), f"wd{sl}", W=[b_wds[sl]])

    APPLY_MODE = os.environ.get("MK_APPLY", "all")

    def ffn(f, prefetched, segs, norms_done=False, stats_done=False, mid_hook=None):
        gi = 0 if f == 0 else 2
        if not prefetched:
            gu_dma(f, 0)
            gu_dma(f, 1)
        seg_tiles = [list(range(st // P, (st + w + P - 1) // P)) for st, w in segs]
        if not norms_done:
            all_tiles = [t for tl in seg_tiles for t in tl]
            if not stats_done:
                norm_stats_all(all_tiles)
            if APPLY_MODE == "all":
                norm_apply_batch(all_tiles, gi)
            elif APPLY_MODE == "2+1":
                norm_apply_batch(seg_tiles[0] + seg_tiles[1], gi)
            else:
                norm_apply_batch(seg_tiles[0], gi)
        wd_dma(f, 0)
        wd_dma(f, 1)
        it = 0
        for c in range(NFC):
            sl = c % 2
            for si, (st, w) in enumerate(segs):
                if c == 0 and not norms_done:
                    if APPLY_MODE == "2+1" and si == 1:
                        norm_apply_batch(seg_tiles[2], gi)
                    elif APPLY_MODE == "seg" and si >= 1:
                        norm_apply_batch(seg_tiles[si], gi)
                g = it % 2
                it += 1
                fns = []
                for which, bk in ((0, g), (1, 2 + g)):
                    for kc in range(KC):
                        fns.append((lambda which, bk, kc, sl, st, w: lambda e: e.matmul(
                            ps[:, bk, 0:w], lhsT=gus[sl][:, which * 1024 + kc * 128: which * 1024 + (kc + 1) * 128],
                            rhs=hT[:, kc, st:st + w], start=(kc == 0), stop=(kc == KC - 1)))(which, bk, kc, sl, st, w))
                S.group("pe", fns, R=[b_gus[sl]] + [b_hT[t] for t in seg_tiles[si]], W=[b_ps[g], b_ps[2 + g]])
                S.op("act", (lambda g, w: lambda e: e.activation(out=silu_t[g][:, 0:w], in_=ps[:, g, 0:w],
                                                                 func=AF.Silu))(g, w),
                     R=[b_ps[g]], W=[b_silu[g]])
                S.op("dve", (lambda g, c, st, w: lambda e: e.tensor_tensor(
                    out=hid[:, c * TOK + st: c * TOK + st + w], in0=ps[:, 2 + g, 0:w], in1=silu_t[g][:, 0:w],
                    op=ALU.mult))(g, c, st, w), R=[b_ps[2 + g], b_silu[g]], W=[b_hid[c]])
            if c + 2 < NFC:
                gu_dma(f, c + 2)
            if mid_hook is not None:
                mid_hook(c)
        pend = None
        it = 0

        def finish(pd):
            g, oc, si = pd
            st, w = segs[si]
            tl = seg_tiles[si]
            nfull = w // P
            rem = w - nfull * P
            fns = [(lambda j, g: lambda e: e.transpose(
                ps[:, 6 + g, j * 128:(j + 1) * 128], dn_t[g][:, j * 128:(j + 1) * 128], identf[:, :]))(j, g)
                for j in range(nfull)]
            if rem:
                fns.append((lambda g: lambda e: e.transpose(
                    ps[:, 6 + g, nfull * 128:(nfull + 1) * 128], dn_t[g][:, nfull * 128:(nfull + 1) * 128],
                    identf[:, :]))(g))
            S.group("pe", fns, R=[b_dn[g], b_c2], W=[b_ps[6 + g]])
            t0 = tl[0]
            S.op("dve", (lambda g, oc: lambda e: e.scalar_tensor_tensor(
                out=xres[:, t0:t0 + nfull, oc * 128:(oc + 1) * 128],
                in0=ps[:, 6 + g, 0:nfull * 128].rearrange("p (j q) -> p j q", j=nfull), scalar=0.5,
                in1=xres[:, t0:t0 + nfull, oc * 128:(oc + 1) * 128], op0=ALU.mult, op1=ALU.add))(g, oc),
                R=[b_ps[6 + g]] + [b_x[t] for t in tl], W=[b_x[t] for t in tl])
            if rem:
                tr = t0 + nfull
                S.op("dve", (lambda g, oc: lambda e: e.scalar_tensor_tensor(
                    out=xres[0:rem, tr, oc * 128:(oc + 1) * 128],
                    in0=ps[0:rem, 6 + g, nfull * 128:(nfull + 1) * 128], scalar=0.5,
                    in1=xres[0:rem, tr, oc * 128:(oc + 1) * 128], op0=ALU.mult, op1=ALU.add))(g, oc),
                    R=[b_ps[6 + g], b_x[tr]], W=[b_x[tr]])

        for oc in range(KC):
            sl = oc % 2
            for si, (st, w) in enumerate(segs):
                g = it % 2
                it += 1
                S.group("pe", [(lambda c, g, sl, st, w: lambda e: e.matmul(
                    ps[:, 4 + g, 0:w], lhsT=wds[sl][:, c * 128:(c + 1) * 128],
                    rhs=hid[:, c * TOK + st: c * TOK + st + w],
                    start=(c == 0), stop=(c == NFC - 1)))(c, g, sl, st, w) for c in range(NFC)],
                    R=[b_wds[sl]] + b_hid, W=[b_ps[4 + g]])
                S.op("act", (lambda g, w: lambda e: e.activation(out=dn_t[g][:, 0:w], in_=ps[:, 4 + g, 0:w],
                                                                 func=AF.Copy))(g, w),
                     R=[b_ps[4 + g]], W=[b_dn[g]])
                if pend is not None:
                    finish(pend)
                pend = (g, oc, si)
            if oc + 2 < KC:
                wd_dma(f, oc + 2)
        finish(pend)

    b_winB = B("winB")

    def load_weights_mix(which):
        if which == "winA":
            tok = S.dma("pool", lambda e: e.dma_start(out=hid[:, 0:768], in_=win_d[0, :, 0:768]), "winA", W=b_win)
            for kc in range(1, KC):
                tok = S.dma("pool", (lambda kc: lambda e: e.dma_start(
                    out=hid[:, kc * DIN:kc * DIN + 768], in_=win_d[kc, :, 0:768]))(kc), "winA")
            for b in b_win:
                b.w = tok
                b.r = []
            return
        if which == "winB":
            for kc in range(KC):
                tokb = S.dma("pool", (lambda kc: lambda e: e.dma_start(
                    out=hid[:, kc * DIN + 768:(kc + 1) * DIN], in_=win_d[kc, :, 768:DIN]))(kc), "winB")
            b_winB.w = tokb
            b_winB.r = []
            return
        name, bufs = "wout", b_wout
        fn = lambda kc: (lambda e: e.dma_start(out=hid[:, 14336 + kc * DM: 14336 + (kc + 1) * DM],
                                               in_=wout_d[kc, :, :]))
        tok = S.dma("pool", fn(0), name, W=bufs)
        for kc in range(1, KC):
            tok = S.dma("pool", fn(kc), name)
        for b in bufs:
            b.w = tok
            b.r = []

    def mix_front(t, gt):
        halo = gt == 0
        sample = gt == NTILES - 1
        cur = gt % 3
        rs = gt % 2
        ub = gt % 2
        UG, b_UG = ugv[ub], b_ugv[ub]
        QT, b_QT = qT[ub], b_qT[ub]
        ropeA, b_ropeA = UG, b_UG
        S.dma("sp", lambda e: e.dma_start(out=ropet[rs][:, :], in_=rope[gt, :, :]), f"rp{rs}", W=[b_ropet[rs]])
        widths = (512, 256, 512, 512)
        offs = (0, 512, 768, 1280)

        def in_proj(nb):
            S.group("pe", [(lambda kc: lambda e: e.matmul(
                ps[:, nb, 0:widths[nb]], lhsT=hT[:, kc, t * P:(t + 1) * P],
                rhs=hid[:, kc * DIN + offs[nb]: kc * DIN + offs[nb] + widths[nb]],
                start=(kc == 0), stop=(kc == KC - 1)))(kc) for kc in range(KC)],
                R=[b_hT[t]] + b_win + ([b_winB] if nb >= 2 else []), W=[b_ps[nb]])

        in_proj(0)
        yield
        in_proj(1)
        yield
        if gt >= NTILES - 2:
            S.op("act", lambda e: e.activation(out=vro[:, :], in_=ps[:, 1, 128:256], func=AF.Copy),
                 R=[b_ps[1]], W=[b_vro])
        S.op("act", lambda e: e.activation(out=vaug[cur][:, :, 0:64],
                                           in_=ps[:, 1, 128:256].rearrange("p (g d) -> p g d", g=2), func=AF.Copy),
             R=[b_ps[1]], W=[b_vaug[cur]])
        yield
        zq = ps[:, 0:2, :].rearrange("p a b -> p (a b)")[:, 0:640]
        z4 = zq.rearrange("p (h u d) -> p h u d", u=2, d=32)
        c2 = ropet[rs][:, 0:64].unsqueeze(1).broadcast_to([P, 10, 64])
        S.op("dve", lambda e: e.tensor_tensor(out=ropeA[:, 0:640].rearrange("p (h d) -> p h d", d=64),
                                              in0=zq.rearrange("p (h d) -> p h d", d=64), in1=c2, op=ALU.mult),
             R=[b_ps[0], b_ps[1], b_ropet[rs]], W=[b_ropeA])
        yield
        rb4 = ropeR[:, 0:640].rearrange("p (h u d) -> p h u d", u=2, d=32)
        S.op("dve", lambda e: e.tensor_tensor(
            out=rb4[:, :, 0, :], in0=z4[:, :, 1, :],
            in1=ropet[rs][:, 64:96].unsqueeze(1).broadcast_to([P, 10, 32]), op=ALU.mult),
            R=[b_ps[0], b_ps[1], b_ropet[rs]], W=[b_ropeR])
        yield
        S.op("dve", lambda e: e.tensor_tensor(
            out=rb4[:, :, 1, :], in0=z4[:, :, 0, :],
            in1=ropet[rs][:, 96:128].unsqueeze(1).broadcast_to([P, 10, 32]), op=ALU.mult),
            R=[b_ps[0], b_ps[1], b_ropet[rs]], W=[b_ropeR])
        yield
        S.op("pool", lambda e: e.tensor_tensor(out=qkr[:, 0:512], in0=ropeA[:, 0:512], in1=ropeR[:, 0:512], op=ALU.add),
             R=[b_ropeA, b_ropeR], W=[b_qkr])
        yield
        in_proj(2)
        yield
        in_proj(3)
        yield
        if gt >= NTILES - 2:
            S.op("pool", lambda e: e.tensor_tensor(out=kro[:, :], in0=ropeA[:, 512:640], in1=ropeR[:, 512:640],
                                                   op=ALU.add), R=[b_ropeA, b_ropeR], W=[b_kro])
            yield
            S.op("pool", lambda e: e.tensor_copy(
                out=qkr[:, 512:768].rearrange("p (g u d) -> p g u d", g=2, u=2),
                in_=kro[:, :].rearrange("p (g d) -> p g d", g=2).unsqueeze(2).broadcast_to([P, 2, 2, 64])),
                R=[b_kro], W=[b_qkr])
            yield
        else:
            for u in range(2):
                S.op("pool", (lambda u: lambda e: e.tensor_tensor(
                    out=qkr[:, 512:768].rearrange("p (g u d) -> p g u d", g=2, u=2)[:, :, u, :],
                    in0=ropeA[:, 512:640].rearrange("p (g d) -> p g d", g=2),
                    in1=ropeR[:, 512:640].rearrange("p (g d) -> p g d", g=2), op=ALU.add))(u),
                    R=[b_ropeA, b_ropeR], W=[b_qkr])
                yield
        S.group("pe", [(lambda j: lambda e: e.transpose(
            bank_bf(4)[:, j * 128:(j + 1) * 128], qkr[:, j * 128:(j + 1) * 128], identb[:, :]))(j)
            for j in (range(4, 6) if halo else range(6))], R=[b_qkr, b_c2], W=[b_ps[4]])
        yield
        S.op("act", lambda e: e.activation(out=kT[cur][:, :, :],
                                           in_=bank_bf(4)[:, 512:768].rearrange("p (g s) -> p g s", g=2), func=AF.Copy),
             R=[b_ps[4]], W=[b_kT[cur]])
        if gt == NTILES - 2:
            out_toks.append(S.dma("sp", lambda e: e.dma_start(out=kwp_o[:, :], in_=kro[:, :]), "o_k", R=[b_kro]))
            out_toks.append(S.dma("sp", lambda e: e.dma_start(out=vwp_o[:, :], in_=vro[:, :]), "o_v", R=[b_vro]))
        if sample:
            for tt in range(4):
                out_toks.append(S.dma("sp", (lambda tt: lambda e: e.dma_start(
                    out=kws_o[:, 124 + tt, :], in_=kro[tt * 16:(tt + 1) * 16, :]))(tt), "o_k", R=[b_kro]))
                out_toks.append(S.dma("sp", (lambda tt: lambda e: e.dma_start(
                    out=vws_o[:, 124 + tt, :], in_=vro[tt * 16:(tt + 1) * 16, :]))(tt), "o_v", R=[b_vro]))
        if halo:
            return
        S.op("act", lambda e: e.activation(out=QT[:, :, :],
                                           in_=bank_bf(4)[:, 0:512].rearrange("p (j q) -> p j q", j=4), func=AF.Copy),
             R=[b_ps[4]], W=[b_QT])
        yield
        S.op("act", lambda e: e.activation(out=gl1[:, :].rearrange("p (a b) -> p a b", a=2), in_=ps[:, 2:4, :],
                                           func=AF.Square), R=[b_ps[2], b_ps[3]], W=[b_gl1])
        yield
        S.op("pool", lambda e: e.tensor_scalar(out=gl1[:, :], in0=gl1[:, :], scalar1=GELU_C2, scalar2=1.0,
                                               op0=ALU.mult, op1=ALU.add), R=[b_gl1], W=[b_gl1])
        yield
        g2 = gl1[:, :].rearrange("p (a b) -> p a b", a=2)
        S.op("dve", lambda e: e.tensor_tensor(out=g2, in0=ps[:, 2:4, :], in1=g2, op=ALU.mult),
             R=[b_gl1, b_ps[2], b_ps[3]], W=[b_gl1])
        yield
        S.op("act", lambda e: e.activation(out=gl1[:, :], in_=gl1[:, :], func=AF.Tanh, scale=GELU_C0),
             R=[b_gl1], W=[b_gl1])
        yield
        for a in range(2):
            S.op("dve", (lambda a: lambda e: e.scalar_tensor_tensor(
                out=UG[:, a * 512:(a + 1) * 512], in0=gl1[:, a * 512:(a + 1) * 512], scalar=1.0,
                in1=ps[:, 2 + a, :], op0=ALU.add, op1=ALU.mult))(a), R=[b_gl1, b_ps[2 + a]], W=[b_UG])
            yield

    def mix_back(t, gt):
        sample = gt == NTILES - 1
        cur = gt % 3
        prv = (gt - 1) % 3
        ub = gt % 2
        UG, b_UG = ugv[ub], b_ugv[ub]
        QT, b_QT = qT[ub], b_qT[ub]
        if not sample:
            for blk in range(2):
                ks = prv if blk == 0 else cur
                for par in range(2):
                    bk = 5 + par
                    pi = blk * 2 + par
                    S.group("pe", [(lambda g, pp, par, bk, ks: lambda e: e.matmul(
                        ps[:, bk, (g * 2 + pp) * 128:(g * 2 + pp + 1) * 128],
                        lhsT=kT[ks][par * 64:par * 64 + 64, g, :],
                        rhs=QT[par * 64:par * 64 + 64, 2 * g + pp, :],
                        start=True, stop=True, skip_group_check=True))(g, pp, par, bk, ks)
                        for g in range(2) for pp in range(2)],
                        R=[b_kT[ks], b_QT], W=[b_ps[bk]])
                    yield
                    S.op("act", (lambda bk, pi: lambda e: e.activation(out=PT[:, pi, :], in_=ps[:, bk, :], func=AF.Exp,
                                                                       scale=0.125))(bk, pi), R=[b_ps[bk]], W=[b_PT[pi]])
                    yield
                    mi = (2 if gt == 1 else 0) if blk == 0 else 1
                    S.op("pool", (lambda pi, mi: lambda e: e.tensor_tensor(
                        out=PT[:, pi, :].rearrange("p (h q) -> p h q", h=4),
                        in0=PT[:, pi, :].rearrange("p (h q) -> p h q", h=4),
                        in1=masks[:, mi, :].unsqueeze(1).broadcast_to([P, 4, 128]), op=ALU.mult))(pi, mi),
                        R=[b_PT[pi], b_const], W=[b_PT[pi]])
                    yield
            for g in range(2):
                fns = []
                for hh in range(4):
                    par, pp = hh % 2, hh // 2
                    for blk in range(2):
                        ks = prv if blk == 0 else cur
                        fns.append((lambda hh, blk, ks, g, par, pp: lambda e: e.matmul(
                            ps[:, 5 + g, hh * 65:(hh + 1) * 65],
                            lhsT=PT[:, blk * 2 + par, (g * 2 + pp) * 128:(g * 2 + pp + 1) * 128],
                            rhs=vaug[ks][:, g, :], start=(blk == 0), stop=(blk == 1),
                            skip_group_check=True))(hh, blk, ks, g, par, pp))
                S.group("pe", fns, R=b_PT + [b_vaug[prv], b_vaug[cur]], W=[b_ps[5 + g]])
                yield
        else:
            fns = []
            for b in range(16):
                for h in range(8):
                    par, hq = h % 2, h // 2
                    fns.append((lambda b, h, par, hq: lambda e: e.matmul(
                        ps[:, par, hq * 64 + b:hq * 64 + 64:16],
                        lhsT=kTc[par * 64:par * 64 + 64, b, h // 4, :],
                        rhs=QT[par * 64:par * 64 + 64, hq, b:64:16],
                        start=True, stop=True, skip_group_check=True))(b, h, par, hq))
            S.group("pe", fns, R=[b_kTc, b_QT], W=[b_ps[0], b_ps[1]])
            yield
            S.group("pe", [(lambda h: lambda e: e.matmul(
                ps[0:64, 2 + h % 2, (h // 2) * 64:(h // 2 + 1) * 64],
                lhsT=kT[cur][(h % 2) * 64:(h % 2) * 64 + 64, h // 4, 0:64],
                rhs=QT[(h % 2) * 64:(h % 2) * 64 + 64, h // 2, 0:64], start=True, stop=True,
                skip_group_check=True))(h) for h in range(8)], R=[b_kT[cur], b_QT], W=[b_ps[2], b_ps[3]])
            yield
            for par in range(2):
                S.op("act", (lambda par: lambda e: e.activation(out=PT[:, par, 0:256], in_=ps[:, par, 0:256],
                                                                func=AF.Exp, scale=0.125))(par),
                     R=[b_ps[par]], W=[b_PT[par]])
                S.op("dve", (lambda par: lambda e: e.tensor_tensor(
                    out=PT[:, par, 0:256].rearrange("p (h q) -> p h q", h=4),
                    in0=PT[:, par, 0:256].rearrange("p (h q) -> p h q", h=4),
                    in1=masks[:, 3, 0:64].unsqueeze(1).broadcast_to([P, 4, 64]), op=ALU.mult))(par),
                    R=[b_PT[par], b_const], W=[b_PT[par]])
                S.op("act", (lambda par: lambda e: e.activation(out=PT[0:64, 2 + par, 0:256], in_=ps[0:64, 2 + par, 0:256],
                                                                func=AF.Exp, scale=0.125))(par),
                     R=[b_ps[2 + par]], W=[b_PT[2 + par]])
                S.op("dve", (lambda par: lambda e: e.tensor_tensor(
                    out=PT[0:64, 2 + par, 0:256].rearrange("p (h q) -> p h q", h=4),
                    in0=PT[0:64, 2 + par, 0:256].rearrange("p (h q) -> p h q", h=4),
                    in1=masks[0:64, 3, 64:128].unsqueeze(1).broadcast_to([64, 4, 64]), op=ALU.mult))(par),
                    R=[b_PT[2 + par], b_const], W=[b_PT[2 + par]])
                yield
            fns = []
            for h in range(8):
                fns.append((lambda h: lambda e: e.matmul(
                    ps[0:65, 4, h * 64:(h + 1) * 64], lhsT=vaug[cur][0:64, h // 4, :],
                    rhs=PT[0:64, 2 + h % 2, (h // 2) * 64:(h // 2 + 1) * 64],
                    start=(h == 0), stop=False, skip_group_check=True))(h))
            for b in range(16):
                for h in range(8):
                    last = (b == 15 and h == 7)
                    fns.append((lambda b, h, last: lambda e: e.matmul(
                        ps[0:65, 4, h * 64 + b:h * 64 + 64:16],
                        lhsT=vaugc[:, b, h // 4, :],
                        rhs=PT[:, h % 2, (h // 2) * 64 + b:(h // 2) * 64 + 64:16],
                        start=False, stop=last, skip_group_check=True))(b, h, last))
            S.group("pe", fns, R=[b_vaug[cur], b_vaugc] + b_PT, W=[b_ps[4]])
            yield
            S.op("act", lambda e: e.activation(out=oTs[0:65, :], in_=ps[0:65, 4, :], func=AF.Copy), R=[b_ps[4]], W=[b_oTs])
            yield
            for g in range(2):
                S.group("pe", [(lambda hh, g: lambda e: e.transpose(
                    ps[0:64, 5 + g, hh * 65:(hh + 1) * 65], oTs[0:65, (4 * g + hh) * 64:(4 * g + hh + 1) * 64],
                    identf[0:65, 0:65]))(hh, g) for hh in range(4)], R=[b_oTs, b_c2], W=[b_ps[5 + g]])
                yield
        rms_stats(UG[:, 512:1024], [b_UG], 512, 6, eps_mul=4.0)
        yield
        if gt >= NTILES - 2:
            S.op("dve", lambda e: e.scalar_tensor_tensor(out=gvn[:, :], in0=UG[:, 512:1024], scalar=stat[:, 6:7],
                                                         in1=g512[:, 0, :], op0=ALU.mult, op1=ALU.mult),
                 R=[b_UG, b_stat[6], b_const], W=[b_gvn])
            yield
            S.op("act", lambda e: e.activation(out=gvnb[:, :], in_=gvn[:, :], func=AF.Copy), R=[b_gvn], W=[b_gvnb])
            yield
        else:
            S.op("dve", lambda e: e.scalar_tensor_tensor(out=gvnb[:, :], in0=UG[:, 512:1024], scalar=stat[:, 6:7],
                                                         in1=g512[:, 0, :], op0=ALU.mult, op1=ALU.mult),
                 R=[b_UG, b_stat[6], b_const], W=[b_gvnb])
            yield
            yield
        if gt == NTILES - 2:
            out_toks.append(S.dma("sp", lambda e: e.dma_start(out=gvp_o[:, :], in_=gvn[:, :]), "o_g", R=[b_gvn]))
        if sample:
            out_toks.append(S.dma("sp", lambda e: e.dma_start(out=gvs_o[:, :], in_=gvn[0:64, :]), "o_g", R=[b_gvn]))
            S.group("pe", [(lambda h: lambda e: e.matmul(
                ps[0:64, 7, h * 64:(h + 1) * 64], lhsT=wsamp[:, h, :], rhs=gvnb[0:64, h * 64:(h + 1) * 64],
                start=True, stop=True, skip_group_check=True))(h) for h in range(8)], R=[b_gvnb, b_c2], W=[b_ps[7]])
            bcol = 8
        else:
            S.group("pe", [(lambda h: lambda e: e.matmul(
                ps[:, 7, h * 64:(h + 1) * 64], lhsT=wsT[:, h, :], rhs=gvnb[:, h * 64:(h + 1) * 64],
                start=True, stop=True, skip_group_check=True))(h) for h in range(8)], R=[b_gvnb, b_c2], W=[b_ps[7]])
            bcol = 0
        S.op("dve", lambda e: e.tensor_tensor(
            out=yg[:, :].rearrange("p (h d) -> p h d", d=64), in0=ps[:, 7, :].rearrange("p (h d) -> p h d", d=64),
            in1=bsT[:, bcol:bcol + 8].unsqueeze(2).broadcast_to([P, 8, 64]), op=ALU.add),
            R=[b_ps[7], b_const], W=[b_yg])
        yield
        for g in range(2):
            pv = ps[:, 5 + g, 0:260].rearrange("p (h e) -> p h e", e=65)
            S.op("dve", (lambda g, pv: lambda e: e.tensor_tensor(
                out=stat[:, 8 + 4 * g:12 + 4 * g].unsqueeze(2), in0=pv[:, :, 64:65],
                in1=esink[:, 4 * g:4 * g + 4].unsqueeze(2), op=ALU.add))(g, pv),
                R=[b_ps[5 + g], b_c2], W=[b_stat[8 + g]])
            S.op("dve", (lambda g: lambda e: e.reciprocal(out=stat[:, 8 + 4 * g:12 + 4 * g],
                                                          in_=stat[:, 8 + 4 * g:12 + 4 * g]))(g),
                 R=[b_stat[8 + g]], W=[b_stat[8 + g]])
            S.op("dve", (lambda g, pv: lambda e: e.tensor_tensor(
                out=ya[:, g * 256:(g + 1) * 256].rearrange("p (h d) -> p h d", d=64), in0=pv[:, :, 0:64],
                in1=stat[:, 8 + 4 * g:12 + 4 * g].unsqueeze(2).broadcast_to([P, 4, 64]), op=ALU.mult))(g, pv),
                R=[b_ps[5 + g], b_stat[8 + g]], W=[b_ya])
            yield
        rms_stats(ya[:, :], [b_ya], 512, 5)
        yield
        S.op("dve", lambda e: e.scalar_tensor_tensor(out=yan[:, :], in0=ya[:, :], scalar=stat[:, 5:6], in1=g512[:, 1, :],
                                                     op0=ALU.mult, op1=ALU.mult), R=[b_ya, b_stat[5], b_const], W=[b_yan])
        yield
        S.op("pool", lambda e: e.tensor_tensor(out=yg[:, :], in0=yg[:, :], in1=UG[:, 0:512], op=ALU.mult),
             R=[b_yg, b_UG], W=[b_yg])
        yield
        rms_stats(yg[:, :], [b_yg], 512, 7, eps_mul=4.0)
        yield
        S.op("dve", lambda e: e.scalar_tensor_tensor(out=ygn[:, :], in0=yg[:, :], scalar=stat[:, 7:8], in1=g512[:, 2, :],
                                                     op0=ALU.mult, op1=ALU.mult), R=[b_yg, b_stat[7], b_const], W=[b_ygn])
        yield

    def mix_tail(t, gt):
        S.group("pe", [(lambda j: lambda e: e.transpose(
            bank_bf(7)[:, j * 128:(j + 1) * 128], (yan if j < 4 else ygn)[:, (j % 4) * 128:(j % 4 + 1) * 128],
            identb[:, :]))(j) for j in range(8)], R=[b_yan, b_ygn, b_c2], W=[b_ps[7]])
        S.op("act", lambda e: e.activation(out=cT[:, :, :], in_=bank_bf(7).rearrange("p (j q) -> p j q", j=8),
                                           func=AF.Copy), R=[b_ps[7]], W=[b_cT])
        yield
        for nh in range(2):
            S.group("pe", [(lambda kc, nh: lambda e: e.matmul(
                ps[:, 7, :], lhsT=cT[:, kc, :],
                rhs=hid[:, 14336 + kc * DM + nh * 512: 14336 + kc * DM + (nh + 1) * 512],
                start=(kc == 0), stop=(kc == KC - 1)))(kc, nh) for kc in range(KC)],
                R=[b_cT] + b_wout, W=[b_ps[7]])
            S.op("dve", (lambda nh: lambda e: e.tensor_tensor(
                out=xres[:, t, nh * 512:(nh + 1) * 512], in0=ps[:, 7, :], in1=xres[:, t, nh * 512:(nh + 1) * 512],
                op=ALU.add))(nh), R=[b_ps[7], b_x[t]], W=[b_x[t]])
            yield

    PATTERN = os.environ.get("MK_PATTERN") or (
        "TF" "BBBB" "T" "BB" "BBBB" "T" "BB" "FF" "BB" "F" "B" "F" "B" "F" "B" "FF" "B" "F" "B" "F" "B" "FF"
        "BB" "F" "B" "F" "BB" "FFFFF")

    def interleave(fr, bk, tl):
        gens = {"F": iter(fr) if fr is not None else None,
                "B": iter(bk) if bk is not None else None,
                "T": iter(tl) if tl is not None else None}

        def step(which):
            g = gens[which]
            if g is None:
                return
            try:
                next(g)
            except StopIteration:
                gens[which] = None

        for ch in PATTERN:
            step(ch)
        while any(g is not None for g in gens.values()):
            step("T")
            step("B")
            step("F")

    emit_const_loads()
    for p in range(NPASS):
        for t in range(NT):
            gt = p * NT + t
            S.dma("pool" if p == 0 else "sp",
                  (lambda t, gt: lambda e: e.dma_start(out=xres[:, t, :], in_=xin[gt, :, :]))(t, gt),
                  (f"xq{t}" if p == 0 else f"x{t}"), W=[b_x[t]])
        if p == 0:
            segs1 = [(0, 384), (384, 384), (768, 384)]
            segs2 = [(128, 384), (512, 384), (896, 256)]
        else:
            segs1 = segs2 = [(0, 384), (384, 384), (768, int(os.environ.get("MK_LASTW", "320")))]
        if p == 0:
            gu_dma(0, 0)
            gu_dma(0, 1)
            norm_stats_all(use_dve=False)
            emit_setup()
            ffn(0, True, segs1, stats_done=True, mid_hook=emit_cache_prep)
        else:
            ffn(0, False, segs1)
        load_weights_mix("winA")
        norm_stats_all()
        load_weights_mix("winB")
        norm_apply_batch(list(range(NT)), 1)
        load_weights_mix("wout")
        gu_dma(1, 0)
        gu_dma(1, 1)
        pend_b = pend_t = None
        for t in range(NT):
            gt = p * NT + t
            interleave(mix_front(t, gt), pend_b, pend_t)
            if t - 2 >= 0 and gt - 2 != 0:
                norm_single(t - 2, 2, 4)
            pend_t = mix_tail(t - 1, gt - 1) if (t >= 1 and gt - 1 != 0) else None
            pend_b = mix_back(t, gt) if gt != 0 else None
        interleave(None, pend_b, pend_t)
        norm_single(NT - 2, 2, 4)
        interleave(None, None, mix_tail(NT - 1, p * NT + NT - 1))
        norm_single(NT - 1, 2, 4)
        ffn(1, True, segs2, norms_done=True)
        norm_stats_all([t for t in range(NT) if p * NT + t != 0], use_dve=False)
        for t in range(NT):
            gt = p * NT + t
            if gt == 0:
                continue
            S.op("dve", (lambda t: lambda e: e.scalar_tensor_tensor(
                out=xres[:, t, :], in0=xres[:, t, :], scalar=stat[:, 16 + t:17 + t], in1=gfin[:, :],
                op0=ALU.mult, op1=ALU.mult))(t), R=[b_x[t], b_stat[16 + t], b_const], W=[b_x[t]])
            out_toks.append(S.dma("sp", (lambda t, gt: lambda e: e.dma_start(
                out=yout[gt - 1, :, :], in_=xres[:, t, :]))(t, gt), f"xo{t}", R=[b_x[t]]))
    out_toks.append(S.dma("sp", lambda e: e.dma_start(out=kws_o[:, 0:124, :], in_=ck_d[:, 4:128, :]), "o_c"))
    out_toks.append(S.dma("sp", lambda e: e.dma_start(out=vws_o[:, 0:124, :], in_=cv_d[:, 4:128, :]), "o_c"))
    S.wait_only("sp", out_toks)

    keys = sorted(S.cnt.keys())
    sems = {k: es.enter_context(nc.semaphore(f"s_{k}")) for k in keys}
    with nc.Block() as block:
        @block.tensor
        def _(e):
            S.emit("pe", e, sems)

        @block.scalar
        def _(e):
            S.emit("act", e, sems)

        @block.vector
        def _(e):
            S.emit("dve", e, sems)

        @block.gpsimd
        def _(e):
            S.emit("pool", e, sems)

        @block.sync
        def _(e):
            S.emit("sp", e, sems)
    es.close()
    return nc


def _rope_tables():
    inv = (10000.0 ** (-np.arange(0, 64, 2, dtype=np.float32) / np.float32(64))).astype(np.float32)
    return inv


def _host_inputs(inp):
    f32 = np.float32
    bf = ml_dtypes.bfloat16
    inv = _rope_tables()
    xp = np.asarray(inp["x_prompt"], f32)
    xsm = np.asarray(inp["x_sample"], f32)
    ck = np.asarray(inp["cache_k_win"], f32)[0].reshape(128, 128, 128)
    cv = np.asarray(inp["cache_v_win"], f32)[0].reshape(128, 128, 128)

    def rep128(v):
        return np.ascontiguousarray(np.broadcast_to(np.asarray(v, f32)[None, :], (128, v.shape[-1])))

    shared = {}
    for f, (g, u, d) in enumerate((("ffn1_gate", "ffn1_up", "ffn1_down"), ("ffn2_gate", "ffn2_up", "ffn2_down"))):
        wg = np.asarray(inp[g], f32)[0].reshape(8, 128, 22, 128)
        wu = np.asarray(inp[u], f32)[0].reshape(8, 128, 22, 128)
        gu = np.stack([wg, wu], 0)
        shared[f"wgu{f}"] = np.ascontiguousarray(gu.transpose(3, 2, 0, 1, 4).reshape(22, 128, 2048))
        wd = np.asarray(inp[d], f32)[0].reshape(22, 128, 8, 128)
        shared[f"wd{f}"] = np.ascontiguousarray(wd.transpose(2, 1, 0, 3).reshape(8, 128, 2816))
    shared["win"] = np.ascontiguousarray(np.asarray(inp["w_in"], f32)[0].reshape(8, 128, 1792))
    shared["wout"] = np.ascontiguousarray(np.asarray(inp["w_out"], f32)[0].reshape(8, 128, 1024))
    shared["gfin"] = rep128(inp["norm_final"])
    shared["gfm"] = np.ascontiguousarray(np.concatenate(
        [np.asarray(inp[k], f32)[0].reshape(8, 128).T for k in ("norm_ffn1", "norm_mix", "norm_ffn2")], axis=1))
    shared["g512"] = np.stack([rep128(inp["gmlp_v_norm"][0]), rep128(inp["norm_attn_out"][0]),
                               rep128(inp["norm_gmlp_out"][0])], 0)
    shared["sinks"] = rep128(inp["attn_sinks"][0])
    ws = np.asarray(inp["gmlp_w_s"], f32)[0]
    shared["wsT"] = np.ascontiguousarray(ws.transpose(2, 0, 1).reshape(128, 1024))
    s_i = np.arange(128)
    shared["trilT"] = (s_i[:, None] <= s_i[None, :]).astype(f32)
    bs = np.asarray(inp["gmlp_b_s"], f32)[0]
    bsT = np.zeros((128, 16), f32)
    bsT[:, 0:8] = bs.T
    bsT[0:64, 8:16] = np.repeat(bs[:, 0:4].T, 16, axis=0)
    shared["bsT"] = bsT
    shared["ws4"] = np.ascontiguousarray(ws[:, 0:4, 0:4].transpose(1, 0, 2).reshape(4, 32))
    t4 = np.arange(4)
    shared["tril4"] = np.ascontiguousarray(
        np.broadcast_to((t4[None, None, :] <= t4[:, None, None]), (4, 8, 4)).astype(f32).reshape(4, 32))
    rep = np.zeros((4, 64), f32)
    for t in range(4):
        rep[t, t * 16:(t + 1) * 16] = 1
    shared["rep"] = rep.astype(bf)
    tb = np.arange(64)
    shared["dmask"] = ((tb[:, None] % 16) == (tb[None, :] % 16)).astype(f32).astype(bf)

    base = np.zeros((128, 5, 128), f32)
    base[:, 0, :] = (s_i[:, None] > s_i[None, :])
    base[:, 1, :] = (s_i[:, None] <= s_i[None, :])
    tq = tb // 16
    base[:, 3, 0:64] = (s_i[:, None] >= (tq[None, :] + 1))
    base[0:64, 3, 64:128] = ((tb[:, None] % 16) == (tb[None, :] % 16)) & ((tb[:, None] // 16) <= (tb[None, :] // 16))
    base[:, 4, :] = np.eye(128, dtype=f32)

    in_maps = []
    for c in range(8):
        b, half = c // 2, c % 2
        xin = np.zeros((NTILES, 128, 1024), f32)
        pos = np.zeros((NTILES, 128), np.int64)
        if half == 1:
            xin[0] = xp[b, 2048 - 128:2048]
            pos[0] = 2048 - 128 + np.arange(128)
        xin[1:17] = xp[b, half * 2048:(half + 1) * 2048].reshape(16, 128, 1024)
        pos[1:17] = (half * 2048 + np.arange(2048)).reshape(16, 128)
        xin[17, 0:64] = xsm[16 * c:16 * c + 16].transpose(1, 0, 2).reshape(64, 1024)
        pos[17, 0:64] = PAST_LEN + np.arange(64) // 16
        ang = (pos.astype(f32)[:, :, None] * inv[None, None, :]).astype(f32)
        co = np.cos(ang.astype(np.float64)).astype(f32)
        si = np.sin(ang.astype(np.float64)).astype(f32)
        rope = np.concatenate([co, co, -si, si], axis=-1).astype(f32)
        m = base.copy()
        if half == 1:
            m[:, 2, :] = base[:, 0, :]
        d = dict(shared)
        d["xin"] = xin
        d["rope"] = np.ascontiguousarray(rope)
        d["masks"] = np.ascontiguousarray(m.reshape(128, 640)).astype(bf)
        d["ck"] = np.ascontiguousarray(ck[16 * c:16 * c + 16])
        d["cv"] = np.ascontiguousarray(cv[16 * c:16 * c + 16])
        in_maps.append(d)
    return in_maps


_NC_CACHE = {}


def kernel(**inputs):
    in_maps = _host_inputs(inputs)
    if "nc" not in _NC_CACHE:
        _NC_CACHE["nc"] = build_program()
    nc = _NC_CACHE["nc"]
    res = run_bass_kernel_spmd(nc, in_maps, core_ids=list(range(8)))
    R = res.results
    f32 = np.float32
    y_prompt = np.zeros((4, 4096, 1024), f32)
    y_sample = np.zeros((128, 4, 1024), f32)
    kwp = np.zeros((1, 4, 128, 2, 64), f32)
    vwp = np.zeros((1, 4, 128, 2, 64), f32)
    kws = np.zeros((1, 128, 128, 2, 64), f32)
    vws = np.zeros((1, 128, 128, 2, 64), f32)
    gvp = np.zeros((1, 4, 128, 8, 64), f32)
    gvs = np.zeros((1, 128, 4, 8, 64), f32)
    for c in range(8):
        b, half = c // 2, c % 2
        r = R[c]
        yo = np.asarray(r["yout"], f32)
        y_prompt[b, half * 2048:(half + 1) * 2048] = yo[0:16].reshape(2048, 1024)
        y_sample[16 * c:16 * c + 16] = yo[16, 0:64].reshape(4, 16, 1024).transpose(1, 0, 2)
        if half == 1:
            kwp[0, b] = np.asarray(r["kwp"], f32).reshape(128, 2, 64)
            vwp[0, b] = np.asarray(r["vwp"], f32).reshape(128, 2, 64)
            gvp[0, b] = np.asarray(r["gvp"], f32).reshape(128, 8, 64)
        kws[0, 16 * c:16 * c + 16] = np.asarray(r["kws"], f32).reshape(16, 128, 2, 64)
        vws[0, 16 * c:16 * c + 16] = np.asarray(r["vws"], f32).reshape(16, 128, 2, 64)
        gvs[0, 16 * c:16 * c + 16] = np.asarray(r["gvs"], f32).reshape(4, 16, 8, 64).transpose(1, 0, 2, 3)
    return (y_prompt, y_sample, kwp, vwp, kws, vws, gvp, gvs)
```
